# Optimizing a Trainium2 kernel written in Bass

```python
import math
import jax, jax.numpy as jnp
from jax import lax
import numpy as np

D_MODEL = 2048
BATCH = 4
SEQ = 2048
DEPTH = 1
DEC_BATCH = 128
DEC_SEQ = 4
PAST_LEN = 16384
PAGE_SIZE = 128

D_CONV = D_MODEL // 2
N_CONV_GROUPS = 16
D_MLSTM = D_MODEL - D_CONV
N_HEADS = 4
HEAD_DIM = D_MLSTM // N_HEADS
CONV_K = 3
D_FF = 5632
CHUNK = 64
ALPHA = (2 * DEPTH) ** 0.25
BETA = (8 * DEPTH) ** -0.25
LN_EPS = 1e-5
IN_COLS = 3 * D_CONV + 4 * D_MLSTM + 2 * N_HEADS

kernel_name = "hymba_shortconv_mlstm_convffn_deepnorm_adaln_step"


def _layernorm(x):
    xf = x.astype(jnp.float32)
    mu = jnp.mean(xf, axis=-1, keepdims=True)
    var = jnp.mean(jnp.square(xf - mu), axis=-1, keepdims=True)
    return (xf - mu) * lax.rsqrt(var + LN_EPS)


def _causal_dwconv(x, buf, w):
    T = x.shape[1]
    xp = jnp.concatenate([buf.astype(x.dtype), x], axis=1)
    y = w[0] * xp[:, 0:T]
    for j in range(1, CONV_K):
        y = y + w[j] * xp[:, j:j + T]
    return y, xp[:, -(CONV_K - 1):]


def _mlstm(q, k, v, log_i, log_f, C0, n0, m0):
    Bsz, T = q.shape[0], q.shape[1]
    L = math.gcd(T, CHUNK)
    nc = T // L

    def to_chunks(a):
        a = a.reshape((Bsz, nc, L) + a.shape[2:])
        a = jnp.moveaxis(a, 1, 0)
        return jnp.swapaxes(a, 2, 3)

    qc, kc, vc = to_chunks(q), to_chunks(k), to_chunks(v)
    ic, fc = to_chunks(log_i), to_chunks(log_f)
    causal = jnp.tril(jnp.ones((L, L), dtype=bool))

    def step(carry, xs):
        C, n, m = carry
        qq, kk, vv, ii, ff = xs
        b = jnp.cumsum(ff, axis=-1)
        a = b + m[..., None]
        dlog = b[..., :, None] - b[..., None, :] + ii[..., None, :]
        dlog = jnp.where(causal, dlog, -jnp.inf)
        mt = jnp.maximum(a, jnp.max(dlog, axis=-1))
        dw = jnp.exp(dlog - mt[..., None])
        inter = jnp.exp(a - mt)
        s = jnp.einsum('bhtd,bhsd->bhts', qq, kk) * dw
        num = jnp.einsum('bhts,bhse->bhte', s, vv) + inter[..., None] * jnp.einsum('bhtd,bhde->bhte', qq, C)
        den = jnp.sum(s, axis=-1) + inter * jnp.einsum('bhtd,bhd->bht', qq, n)
        h = num / jnp.maximum(jnp.abs(den), jnp.exp(-mt))[..., None]
        m_new = mt[..., -1]
        wc = jnp.exp(b[..., -1:] - b + ii - m_new[..., None])
        dc = jnp.exp(a[..., -1] - m_new)
        C_new = dc[..., None, None] * C + jnp.einsum('bhs,bhsd,bhse->bhde', wc, kk, vv)
        n_new = dc[..., None] * n + jnp.einsum('bhs,bhsd->bhd', wc, kk)
        return (C_new, n_new, m_new), h

    (C, n, m), hs = lax.scan(step, (C0, n0, m0), (qc, kc, vc, ic, fc))
    h = jnp.swapaxes(hs, 2, 3)
    h = jnp.moveaxis(h, 0, 1).reshape(Bsz, T, N_HEADS, HEAD_DIM)
    return h, C, n, m


def _layer(x, c, conv_buf, C0, n0, m0, ffn_buf, w_ada, b_ada, w_in, b_gate, w_conv, w_mh_norm,
           w_out, ln1_g, ln1_b, w_up, w_ffn_conv, w_down, ln2_g, ln2_b):
    Bsz, T, _ = x.shape
    mod = jax.nn.silu(c.astype(jnp.float32)) @ w_ada + b_ada
    sh1, sc1, g1, sh2, sc2, g2 = [t[:, None, :] for t in jnp.split(mod, 6, axis=-1)]

    u = _layernorm(x) * (1.0 + sc1) + sh1
    p = u @ w_in
    cuts = [D_CONV, 2 * D_CONV, 3 * D_CONV,
            3 * D_CONV + D_MLSTM, 3 * D_CONV + 2 * D_MLSTM,
            3 * D_CONV + 3 * D_MLSTM, 3 * D_CONV + 4 * D_MLSTM]
    bg, cg, hc, q, k, v, o, gates = jnp.split(p, cuts, axis=-1)

    yc, conv_new = _causal_dwconv(cg * hc, conv_buf, w_conv)
    y_conv = bg * yc

    gates = gates + b_gate
    log_i = gates[..., :N_HEADS]
    log_f = jax.nn.log_sigmoid(gates[..., N_HEADS:])
    qh = q.reshape(Bsz, T, N_HEADS, HEAD_DIM).astype(jnp.float32)
    kh = k.reshape(Bsz, T, N_HEADS, HEAD_DIM).astype(jnp.float32) * (HEAD_DIM ** -0.5)
    vh = v.reshape(Bsz, T, N_HEADS, HEAD_DIM).astype(jnp.float32)
    hm, C1, n1, m1 = _mlstm(qh, kh, vh, log_i.astype(jnp.float32), log_f.astype(jnp.float32),
                            C0.astype(jnp.float32), n0.astype(jnp.float32), m0.astype(jnp.float32))
    hm = _layernorm(hm).reshape(Bsz, T, D_MLSTM) * w_mh_norm
    y_mlstm = jax.nn.sigmoid(o) * hm

    mix = jnp.concatenate([y_conv, y_mlstm], axis=-1) @ w_out
    x = _layernorm(ALPHA * x + (1.0 + g1) * mix) * ln1_g + ln1_b

    u2 = _layernorm(x) * (1.0 + sc2) + sh2
    up = u2 @ w_up
    a, g = jnp.split(up, 2, axis=-1)
    ac, ffn_new = _causal_dwconv(a, ffn_buf, w_ffn_conv)
    y = (jax.nn.silu(ac) * g) @ w_down
    x = _layernorm(ALPHA * x + (1.0 + g2) * y) * ln2_g + ln2_b
    return x, conv_new, C1, n1, m1, ffn_new


def setup_inputs(seed: int = 0) -> dict:
    key = jax.random.key(seed)
    ks = jax.random.split(key, 24)
    nrm = lambda k, s: jax.random.normal(k, s, dtype=jnp.float32)
    f_bias = jnp.linspace(3.0, 6.0, N_HEADS, dtype=jnp.float32)
    i_bias = jnp.full((N_HEADS,), -2.0, dtype=jnp.float32)
    b_gate = jnp.concatenate([i_bias, f_bias])[None, :] + 0.1 * nrm(ks[10], (DEPTH, 2 * N_HEADS))
    return {
        "x_prompt": nrm(ks[0], (BATCH, SEQ, D_MODEL)),
        "x_sample": nrm(ks[1], (DEC_BATCH, DEC_SEQ, D_MODEL)),
        "c_prompt": nrm(ks[2], (BATCH, D_MODEL)),
        "c_sample": nrm(ks[3], (DEC_BATCH, D_MODEL)),
        "state_conv": nrm(ks[4], (DEPTH, DEC_BATCH, CONV_K - 1, D_CONV)),
        "state_mlstm_C": 0.1 * nrm(ks[5], (DEPTH, DEC_BATCH, N_HEADS, HEAD_DIM, HEAD_DIM)),
        "state_mlstm_n": nrm(ks[6], (DEPTH, DEC_BATCH, N_HEADS, HEAD_DIM)),
        "state_mlstm_m": 0.5 * nrm(ks[7], (DEPTH, DEC_BATCH, N_HEADS)),
        "state_ffn_conv": nrm(ks[8], (DEPTH, DEC_BATCH, CONV_K - 1, D_FF)),
        "w_ada": 0.1 * D_MODEL ** -0.5 * nrm(ks[9], (DEPTH, D_MODEL, 6 * D_MODEL)),
        "b_ada": 0.02 * nrm(ks[11], (DEPTH, 6 * D_MODEL)),
        "w_in": D_MODEL ** -0.5 * nrm(ks[12], (DEPTH, D_MODEL, IN_COLS)),
        "b_gate": b_gate,
        "w_conv": CONV_K ** -0.5 * nrm(ks[13], (DEPTH, CONV_K, D_CONV)),
        "w_mh_norm": 1.0 + 0.02 * nrm(ks[14], (DEPTH, D_MLSTM)),
        "w_out": BETA * D_MODEL ** -0.5 * nrm(ks[15], (DEPTH, D_MODEL, D_MODEL)),
        "ln1_g": 1.0 + 0.02 * nrm(ks[16], (DEPTH, D_MODEL)),
        "ln1_b": 0.02 * nrm(ks[17], (DEPTH, D_MODEL)),
        "w_up": D_MODEL ** -0.5 * nrm(ks[18], (DEPTH, D_MODEL, 2 * D_FF)),
        "w_ffn_conv": CONV_K ** -0.5 * nrm(ks[19], (DEPTH, CONV_K, D_FF)),
        "w_down": BETA * D_FF ** -0.5 * nrm(ks[20], (DEPTH, D_FF, D_MODEL)),
        "ln2_g": 1.0 + 0.02 * nrm(ks[21], (DEPTH, D_MODEL)),
        "ln2_b": 0.02 * nrm(ks[22], (DEPTH, D_MODEL)),
    }


def reference(x_prompt, x_sample, c_prompt, c_sample, state_conv, state_mlstm_C, state_mlstm_n,
              state_mlstm_m, state_ffn_conv, w_ada, b_ada, w_in, b_gate, w_conv, w_mh_norm, w_out,
              ln1_g, ln1_b, w_up, w_ffn_conv, w_down, ln2_g, ln2_b):
    Bp = x_prompt.shape[0]
    f32 = jnp.float32
    xp = x_prompt
    xs = x_sample
    pc, pC, pn, pm, pf = [], [], [], [], []
    sc, sC, sn, sm, sf = [], [], [], [], []
    for l in range(DEPTH):
        wl = (w_ada[l], b_ada[l], w_in[l], b_gate[l], w_conv[l], w_mh_norm[l], w_out[l],
              ln1_g[l], ln1_b[l], w_up[l], w_ffn_conv[l], w_down[l], ln2_g[l], ln2_b[l])
        xp, c1, C1, n1, m1, f1 = _layer(
            xp, c_prompt,
            jnp.zeros((Bp, CONV_K - 1, D_CONV), f32),
            jnp.zeros((Bp, N_HEADS, HEAD_DIM, HEAD_DIM), f32),
            jnp.zeros((Bp, N_HEADS, HEAD_DIM), f32),
            jnp.zeros((Bp, N_HEADS), f32),
            jnp.zeros((Bp, CONV_K - 1, D_FF), f32), *wl)
        pc.append(c1); pC.append(C1); pn.append(n1); pm.append(m1); pf.append(f1)
        xs, c2, C2, n2, m2, f2 = _layer(
            xs, c_sample, state_conv[l], state_mlstm_C[l], state_mlstm_n[l], state_mlstm_m[l],
            state_ffn_conv[l], *wl)
        sc.append(c2); sC.append(C2); sn.append(n2); sm.append(m2); sf.append(f2)
    y_prompt = xp.astype(x_prompt.dtype)
    y_sample = xs.astype(x_sample.dtype)
    return (y_prompt, y_sample,
            jnp.stack(pc), jnp.stack(pC), jnp.stack(pn), jnp.stack(pm), jnp.stack(pf),
            jnp.stack(sc), jnp.stack(sC), jnp.stack(sn), jnp.stack(sm), jnp.stack(sf))
```

```python
import numpy as np
from contextlib import ExitStack
import concourse.bass as bass
import concourse.mybir as mybir
from concourse.bass_utils import run_bass_kernel_spmd

F32 = mybir.dt.float32
BF16 = mybir.dt.bfloat16
ALU = mybir.AluOpType
AF = mybir.ActivationFunctionType
AX = mybir.AxisListType

D = 2048
KC = 16
DFF = 5632
JF = 44
NH = 4
HD = 256
NPASS = 4
TPP = 512
TS = 64
ALPHA = float(2 ** 0.25)
EPS = 1e-5
NEG = -1.0e30
INCOLS = 7176

C_ID, C_NEGC, C_NEGCT, C_TRI, C_ONE = 0, 128, 256, 384, 512
C_SNEGC, C_SNEGCT, C_SNEGB, C_STRI = 640, 704, 768, 832
C_BMREP, C_BM = 896, 1920
NCON = 1936


def make_consts():
    c = np.zeros((128, NCON), np.float32)
    i = np.arange(128)
    c[:, C_ID:C_ID + 128] = np.eye(128)
    c[:, C_NEGC:C_NEGC + 128] = np.where(i[None, :] <= i[:, None], 0.0, NEG)
    c[:, C_NEGCT:C_NEGCT + 128] = np.where(i[:, None] <= i[None, :], 0.0, NEG)
    c[:, C_TRI:C_TRI + 128] = (i[:, None] <= i[None, :]).astype(np.float32)
    c[:, C_ONE:C_ONE + 128] = 1.0
    j = np.arange(64)
    same = (j[:, None] // 4) == (j[None, :] // 4)
    c[:64, C_SNEGC:C_SNEGC + 64] = np.where(same & (j[None, :] <= j[:, None]), 0.0, NEG)
    c[:64, C_SNEGCT:C_SNEGCT + 64] = np.where(same & (j[:, None] <= j[None, :]), 0.0, NEG)
    c[:64, C_SNEGB:C_SNEGB + 64] = np.where(same, 0.0, NEG)
    c[:64, C_STRI:C_STRI + 64] = (same & (j[:, None] <= j[None, :])).astype(np.float32)
    bm = ((j[None, :] // 4) == np.arange(16)[:, None]).astype(np.float32)
    c[:, C_BMREP:C_BMREP + 1024] = np.tile(bm.reshape(1, 1024), (128, 1))
    c[:64, C_BM:C_BM + 16] = bm.T
    return c


class Buf:
    def __init__(self, name, t):
        self.name = name
        self.t = t
        self.w = []
        self.r = []
        self.dsem = None
        self.dcount = 0
        self.excl = False

    def __getitem__(self, k):
        return self.t[k]


class Prog:
    ENG = ["pe", "act", "dve", "pool", "sp"]

    def __init__(self, nc, stack):
        self.nc = nc
        self.stack = stack
        self.sem = {e: stack.enter_context(nc.semaphore("sem_" + e)) for e in self.ENG}
        self.cnt = {e: 0 for e in self.ENG}
        self.seen = {e: {} for e in self.ENG}
        self.ops = {e: [] for e in self.ENG}
        self.cur = 16512
        self.limit = 229344
        self.nalloc = 0
        self.out_toks = []
        self.dead = False

    def sbuf(self, name, shape, dt, at=None):
        esz = 4 if dt == F32 else 2
        n = 1
        for s in shape[1:]:
            n *= s
        size = n * esz
        if at is None:
            off = (self.cur + 31) // 32 * 32
            self.cur = off + size
            assert self.cur <= self.limit, ("SBUF overflow", name, self.cur)
        else:
            off = at
        self.nalloc += 1
        t = self.nc.alloc_sbuf_tensor_at("%s_%d" % (name, self.nalloc), list(shape), dt, offset=off)
        b = Buf(name, t)
        b.off = off
        b.size = size
        return b

    def psum(self, name, shape, dt=F32):
        t = self.stack.enter_context(self.nc.psum_tensor(name, list(shape), dt))
        b = Buf(name, t)
        b.excl = True
        return b

    def _wait(self, e, tok):
        sem, val = tok
        key = id(sem)
        if self.seen[e].get(key, 0) >= val:
            return
        self.seen[e][key] = val
        self.ops[e].append(("wait", sem, val))

    def _deps(self, e, reads, writes):
        for b in reads:
            for tok in b.w:
                self._wait(e, tok)
            if b.excl:
                for tok in b.r:
                    self._wait(e, tok)
        for b in writes:
            for tok in b.w:
                self._wait(e, tok)
            for tok in b.r:
                self._wait(e, tok)

    def _reg(self, tok, reads, writes):
        for b in reads:
            if len(b.r) > 24:
                d = {}
                for s, v in b.r:
                    if d.get(id(s), (None, 0))[1] < v:
                        d[id(s)] = (s, v)
                b.r = list(d.values())
            b.r.append(tok)
        for b in writes:
            b.w = [tok]
            b.r = []

    def op(self, e, fn, reads=(), writes=(), signal=True):
        if self.dead:
            return
        self._deps(e, reads, writes)
        if signal:
            self.cnt[e] += 1
            tok = (self.sem[e], self.cnt[e])
            self.ops[e].append(("op", fn, self.sem[e], 1))
            self._reg(tok, reads, writes)
        else:
            self.ops[e].append(("op", fn, None, 0))

    def dma(self, e, out_ap, in_ap, reads=(), writes=(), is_out=False, nodeps=False, owner=None):
        if self.dead:
            return
        if not nodeps:
            self._deps(e, reads, writes)
        if owner is None:
            owner = (list(writes) + list(reads))[0]
        if owner.dsem is None:
            owner.dsem = {}
            owner.dcount = {}
        if e not in owner.dsem:
            owner.dsem[e] = self.stack.enter_context(self.nc.semaphore("ds%d_%s_%s" % (self.nalloc, owner.name, e)))
            owner.dcount[e] = 0
            self.nalloc += 1
        owner.dcount[e] += 16
        dsem = owner.dsem[e]
        tok = (dsem, owner.dcount[e])
        self.ops[e].append(("op", lambda eng: eng.dma_start(out=out_ap, in_=in_ap), dsem, 16))
        self._reg(tok, reads, writes)
        if is_out:
            self.out_toks.append(tok)
        return tok

    def handoff(self, old, new):
        if self.dead:
            return
        d = {}
        for b in old:
            for s_, v in b.w + b.r:
                if d.get(id(s_), (None, 0))[1] < v:
                    d[id(s_)] = (s_, v)
        for b in new:
            dd = dict(d)
            for s_, v in b.r:
                if dd.get(id(s_), (None, 0))[1] < v:
                    dd[id(s_)] = (s_, v)
            b.r = list(dd.values())

    def emit(self):
        nc = self.nc
        ops = self.ops
        last = {}
        for s, v in self.out_toks:
            if last.get(id(s), (None, 0))[1] < v:
                last[id(s)] = (s, v)
        for tok in last.values():
            self._wait("sp", tok)

        def run(eng, lst):
            for o in lst:
                if o[0] == "wait":
                    eng.wait_ge(o[1], o[2])
                else:
                    ins = o[1](eng)
                    if o[2] is not None:
                        ins.then_inc(o[2], o[3])

        with nc.Block() as block:
            @block.tensor
            def _(eng):
                run(eng, ops["pe"])

            @block.scalar
            def _(eng):
                run(eng, ops["act"])

            @block.vector
            def _(eng):
                run(eng, ops["dve"])

            @block.gpsimd
            def _(eng):
                run(eng, ops["pool"])

            @block.sync
            def _(eng):
                run(eng, ops["sp"])


class _Stop(Exception):
    pass


STOP = None


def _ck(P, name):
    if STOP == name:
        P.dead = True


def build():
    nc = bass.Bass("TRN2", target_bir_lowering=False)

    def din(name, shape):
        return nc.dram_tensor(name, list(shape), F32, kind="ExternalInput").ap()

    def dout(name, shape):
        return nc.dram_tensor(name, list(shape), F32, kind="ExternalOutput").ap()

    xp = din("xp", [2048, D]); xs = din("xs", [TS, D])
    cT_d = din("cT", [128, KC, 209])
    con_d = din("con", [128, NCON])
    pm_d = din("pm", [128, 1])
    m0tok_d = din("m0tok", [TS, 4]); m0rep_d = din("m0rep", [128, 64])
    C0_d = din("C0", [16, 4, 256, 256]); n0T_d = din("n0T", [128, 128])
    sconvT_d = din("sconvT", [128, 8 * 32]); sfcT_d = din("sfcT", [128, JF * 32])
    w_ada = din("w_ada", [D, 6 * D]); badaT_d = din("badaT", [128, 96]); badarep_d = din("badarep", [128, 4096])
    w_in = din("w_in", [D, INCOLS]); bgate_d = din("bgate", [128, 8])
    wconvT_d = din("wconvT", [128, 24]); wmhT_d = din("wmhT", [128, 8])
    w_out = din("w_out", [D, D]); w_up = din("w_up", [D, 2 * DFF]); wfcT_d = din("wfcT", [128, JF * 3])
    w_down = din("w_down", [DFF, D])
    ln_d = din("lnrep", [4, 128, D])

    y_p = dout("y_p", [1024, D]); y_s = dout("y_s", [TS, D])
    o_pconv = dout("o_pconv", [128, 16]); o_pC = dout("o_pC", [128, 4, 2, 256]); o_pn = dout("o_pn", [128, 8])
    o_pm = dout("o_pm", [128, 4]); o_pfc = dout("o_pfc", [128, JF * 2])
    o_sconv = dout("o_sconv", [128, 8 * 32]); o_sC = dout("o_sC", [16, 4, 256, 256]); o_sn = dout("o_sn", [128, 128])
    o_sm = dout("o_sm", [TS, 4]); o_sfc = dout("o_sfc", [128, JF * 32])

    w_ada_r = w_ada.rearrange("(k p) c -> p k c", p=128)
    w_in_r = w_in.rearrange("(k p) c -> p k c", p=128)
    w_out_r = w_out.rearrange("(k p) c -> p k c", p=128)
    w_up_r = w_up.rearrange("(k p) c -> p k c", p=128)
    w_down_r = w_down.rearrange("(k p) c -> p k c", p=128)

    with ExitStack() as st:
        P = Prog(nc, st)
        CON = P.sbuf("con", [128, NCON], F32)
        identb = P.sbuf("identb", [128, 128], BF16)
        modT = P.sbuf("modT", [128, 4, KC, 17], F32)
        G = [P.sbuf("G%d" % i, [128, D], BF16) for i in range(4)]
        small = P.sbuf("small", [128, 512], F32)
        S_BADA, S_BG, S_WCV, S_WMH, S_WFC, S_EPS, S_M0T, S_M0R = 0, 96, 104, 128, 136, 268, 272, 276
        S_PM, S_PMOFF = 344, 345
        wg = P.sbuf("wg", [128, KC, 8], BF16)
        Cst = P.sbuf("Cst", [128, 4, 2, 257], F32)
        Cbf = P.sbuf("Cbf", [128, 4, 2, 257], BF16)
        FM = P.sbuf("FM", [128, 16], F32)
        prodH = P.sbuf("prodH", [128, 8, 2], F32)
        aH = P.sbuf("aH", [128, JF, 2], F32)
        sconvT = P.sbuf("sconvT", [128, 8, 16, 2], F32)
        sfcT = P.sbuf("sfcT", [128, JF, 16, 2], F32)
        n0T = P.sbuf("n0T", [128, 16, 4, 2], F32)
        sconv_o, sfc_o = sconvT, sfcT
        sn_o = Buf("sn_o", n0T.t)
        NTL = 4
        GT = P.sbuf("GT", [128, NTL, 8], F32)
        gs = {nm: P.sbuf("g_" + nm, [128, NTL, 4], F32) for nm in ["F", "g", "M", "inter", "wc", "dcr", "en", "mtok"]}
        dcrs = P.sbuf("dcrs", [128, 4, 16], F32)
        DT = P.sbuf("DT", [128, NTL, 4, 128], BF16)
        tmp4 = P.sbuf("tmp4", [128, 4, 128], F32)
        diag4 = P.sbuf("diag4", [128, 4, 128], F32)
        sc = P.sbuf("sc", [128, 64], F32)
        xn = P.sbuf("xn", [128, D], BF16)
        stt = P.sbuf("stt", [128, 4, 6], F32)
        mv = P.sbuf("mv", [128, 4], F32)
        etmp = [P.sbuf("etmp%d" % i, [128, 512], F32) for i in range(2)]
        brep = etmp
        qTs = P.sbuf("qTs", [128, 8, TS], BF16)
        sigs = P.sbuf("sigs", [128, 8, TS], BF16)
        As_sb = P.sbuf("As_sb", [TS, 4, 257], F32)
        NTOK = TPP
        R = P.sbuf("R", [128, NTL, D], F32)
        uT = P.sbuf("uT", [128, KC, NTOK], BF16)
        uTf = P.sbuf("uTf", [128, 2, D], F32, at=uT.off)
        Rt = [Buf("Rt%d" % i, None) for i in range(NTL)]
        WB = [P.sbuf("wb%d" % i, [128, KC, 512], BF16) for i in range(2)]
        zbase = P.cur
        yT = P.sbuf("yT", [128, KC, NTOK], BF16)
        sT = P.sbuf("sT", [128, KC, 209], BF16, at=yT.off)
        kw = P.sbuf("kw", [128, 1, NTL, 256], BF16)
        vext = P.sbuf("vext", [128, 1, NTL, 257], BF16)
        SwT = P.sbuf("SwT", [128, 128], BF16)
        Bs = P.sbuf("Bs", [128, 257], F32)
        numx = P.sbuf("numx", [128, 257], F32)
        hn = P.sbuf("hn", [128, 256], BF16)
        cbase = P.cur
        qT = P.sbuf("qT", [128, 2, NTOK], BF16)
        kT = P.sbuf("kT", [128, 2, NTOK], BF16)
        sigo = P.sbuf("sigo", [128, 2, NTOK], BF16)
        Csb = P.sbuf("Csb", [128, 512], F32)
        prod = P.sbuf("prod", [128, 520], F32)
        ctmp = P.sbuf("ctmp", [128, 512], F32)
        psx = P.sbuf("psx", [128, 16, 6], F32)
        cend = P.cur
        P.cur = cbase
        C0fH = [P.sbuf("C0f%d" % i, [128, 2, 2, 257], F32) for i in range(2)]
        C0bH = [P.sbuf("C0b%d" % i, [128, 2, 2, 257], BF16) for i in range(2)]
        QmR = [P.sbuf("Qm%d" % i, [128, 8, TS], BF16) for i in range(2)]
        kwmR = [P.sbuf("kwm%d" % i, [TS, 256], BF16) for i in range(2)]
        zend = max(P.cur, cend)
        mixer_bufs = [yT, kw, vext, qT, kT, sigo, SwT, Bs, numx, hn, Csb, prod, ctmp, psx] + C0fH + C0bH + QmR + kwmR
        conv_bufs = [Csb, prod, ctmp, psx]
        samp_bufs = C0fH + C0bH + QmR + kwmR
        head_bufs = [qT, kT, sigo]
        P.cur = zbase
        zT = P.sbuf("zT", [128, JF, NTOK], BF16)
        abuf = P.sbuf("abuf", [128, 520], F32)
        asx = P.sbuf("asx", [128, 16, 6], F32)
        ffn_bufs = [zT, abuf, asx]
        P.cur = max(P.cur, zend)
        print("SBUF used per partition:", P.cur, "of", P.limit)

        PA = [P.psum("pa%d" % i, [128, 512], F32) for i in range(6)]
        PT = [P.psum("pt%d" % i, [128, 8, 128], BF16) for i in range(2)]
        rot = {"a": 0, "t": 0, "w": 0, "l": 0, "b": 0, "e": 0}

        def nxt(key, lst):
            rot[key] = (rot[key] + 1) % len(lst)
            return lst[rot[key]]

        def con(c0, n, rows=128):
            return CON[0:rows, c0:c0 + n]

        def mm(ps, out_ap, pairs, reads):
            n = len(pairs)
            for i, (l, r) in enumerate(pairs):
                P.op("pe", lambda e, l=l, r=r, i=i: e.matmul(out_ap, lhsT=l, rhs=r, start=(i == 0), stop=(i == n - 1)),
                     reads=reads, writes=[ps], signal=(i == n - 1))

        def load_w(wb, pieces):
            first = True
            for c0, src in pieces:
                n = src.shape[-1]
                kk = src.shape[1]
                ksp = [(0, kk)] if kk * n <= 2048 else [(0, kk // 2), (kk // 2, kk)]
                for (ka, kb) in ksp:
                    P.dma("pool", wb[:, ka:kb, c0:c0 + n], src[:, ka:kb, :], writes=[wb], nodeps=not first)
                    first = False

        wcache = nc.dram_tensor("wcache", [62, 128, KC * 512], BF16).ap()
        cbufs = [Buf("cache%d" % i, None) for i in range(62)]
        ust = {"u": 0, "first": True}

        def get_unit(pieces):
            wb = nxt("w", WB)
            u = ust["u"]
            ust["u"] += 1
            flat = wb[:, :, :].rearrange("p k c -> p (k c)")
            if ust["first"]:
                load_w(wb, pieces)
                P.dma("sp", wcache[u], flat, reads=[wb], writes=[cbufs[u]], owner=wb)
            else:
                h = KC * 256
                P.dma("sp", flat[:, 0:h], wcache[u][:, 0:h], reads=[cbufs[u]], writes=[wb], owner=wb)
                P.dma("sp", flat[:, h:2 * h], wcache[u][:, h:2 * h], reads=[cbufs[u]], writes=[wb], owner=wb, nodeps=True)
            return wb

        BODY_START = True
        P.dma("sp", CON[:, :], con_d, writes=[CON])
        P.op("dve", lambda e: e.tensor_copy(out=identb[:, :], in_=CON[:, C_ID:C_ID + 128]), reads=[CON], writes=[identb])
        P.op("pool", lambda e: e.memset(small[:, :], 0.0), writes=[small])
        P.dma("sp", small[:, S_BADA:S_BADA + 96], badaT_d, writes=[small])
        P.dma("sp", small[:, S_BG:S_BG + 8], bgate_d, writes=[small])
        P.dma("sp", small[:, S_WCV:S_WCV + 24], wconvT_d, writes=[small])
        P.dma("sp", small[:, S_WMH:S_WMH + 8], wmhT_d, writes=[small])
        P.dma("sp", small[:, S_WFC:S_WFC + 132], wfcT_d, writes=[small])
        P.dma("sp", small[0:TS, S_M0T:S_M0T + 4], m0tok_d, writes=[small])
        P.dma("sp", small[:, S_M0R:S_M0R + 64], m0rep_d, writes=[small])
        P.op("dve", lambda e: e.memset(small[:, S_EPS:S_EPS + 1], EPS), reads=[], writes=[small])
        P.dma("sp", small[:, S_PM:S_PM + 1], pm_d, writes=[small])
        P.op("dve", lambda e: e.tensor_scalar(out=small[:, S_PMOFF:S_PMOFF + 1], in0=small[:, S_PM:S_PM + 1], scalar1=-1.0, scalar2=3.0e4,
                                              op0=ALU.add, op1=ALU.mult), reads=[small], writes=[small])
        P.dma("sp", sconvT[:, :, :, :].rearrange("p a b c -> p (a b c)"), sconvT_d, writes=[sconvT])
        P.dma("sp", sfcT[:, :, :, :].rearrange("p a b c -> p (a b c)"), sfcT_d, writes=[sfcT])
        P.dma("sp", n0T[:, :, :, :].rearrange("p a b c -> p (a b c)"), n0T_d, writes=[n0T])
        P.op("dve", lambda e: e.memset(Cst[:, :, :, :], 0.0), writes=[Cst])
        P.op("dve", lambda e: e.memset(Cbf[:, :, :, :], 0.0), writes=[Cbf])
        P.op("dve", lambda e: e.memset(FM[:, :], 0.0), writes=[FM])
        P.op("dve", lambda e: e.memset(prodH[:, :, :], 0.0), writes=[prodH])
        P.op("dve", lambda e: e.memset(aH[:, :, :], 0.0), writes=[aH])
        epsb = small[:, S_EPS:S_EPS + 1]
        Rflat = R[:, :, :].rearrange("p a b -> p (a b)")
        P.dma("sp", Rflat[:, 0:KC * 209], cT_d.rearrange("p k n -> p (k n)"), writes=Rt)
        P.op("act", lambda e: e.activation(out=sT[:, :, :].rearrange("p k n -> p (k n)"), in_=Rflat[:, 0:KC * 209], func=AF.Silu),
             reads=Rt, writes=[sT])
        load_w(wg, [(0, w_in_r[:, :, 7168:7176])])
        gi_of_group = {0: 0, 1: 1, 3: 2, 4: 3}
        def ada_units(ulist):
            for u in ulist:
                g, sub = u // 4, u % 4
                wb = nxt("w", WB)
                load_w(wb, [(0, w_ada_r[:, :, u * 512:(u + 1) * 512])])
                if g in gi_of_group:
                    gi = gi_of_group[g]
                    for cc in range(4):
                        j = sub * 4 + cc
                        ps = nxt("a", PA)
                        mm(ps, ps[:, 0:17], [(wb[:, k, cc * 128:(cc + 1) * 128], sT[:, k, 0:17]) for k in range(KC)], [wb, sT])
                        bcol = small[:, S_BADA + g * 16 + j:S_BADA + g * 16 + j + 1]
                        addc = 1.0 if g in (1, 4) else 0.0
                        P.op("dve", lambda e, ps=ps, gi=gi, j=j, bcol=bcol, addc=addc: e.tensor_scalar(
                            out=modT[:, gi, j, :], in0=ps[:, 0:17], scalar1=bcol, scalar2=addc, op0=ALU.add, op1=ALU.add),
                            reads=[ps, small], writes=[modT])
                else:
                    gq = 0 if g == 2 else 1
                    br = nxt("b", brep)
                    P.dma("sp", br[:, :], badarep_d[:, gq * 2048 + sub * 512:gq * 2048 + (sub + 1) * 512], writes=[br])
                    for smp in range(2):
                        rows = 128 if smp == 0 else TS
                        c0 = 17 if smp == 0 else 145
                        ps = nxt("a", PA)
                        mm(ps, ps[0:rows, :], [(sT[:, k, c0:c0 + rows], wb[:, k, :]) for k in range(KC)], [wb, sT])
                        Gb = G[gq * 2 + smp]
                        P.op("dve", lambda e, ps=ps, rows=rows, Gb=Gb, br=br, sub=sub: e.scalar_tensor_tensor(
                            out=Gb[0:rows, sub * 512:(sub + 1) * 512], in0=ps[0:rows, :], scalar=1.0, in1=br[0:rows, :],
                            op0=ALU.add, op1=ALU.add), reads=[ps, br], writes=[Gb])
                yield

        for _ in ada_units(range(0, 8)):
            pass
        ada_gen = ada_units(range(8, 24))

        def ln_stats(src_buf, ti, rows):
            for q in range(4):
                P.op("dve", lambda e, q=q: e.bn_stats(out=stt[0:rows, q, :], in_=src_buf[0:rows, ti, q * 512:(q + 1) * 512]),
                     reads=[Rt[ti]], writes=[stt])
            P.op("dve", lambda e: e.bn_aggr(out=mv[0:rows, 0:2], in_=stt[0:rows, :, :].rearrange("p a b -> p (a b)")),
                 reads=[stt], writes=[mv])
            P.op("act", lambda e: e.activation(out=mv[0:rows, 2:3], in_=mv[0:rows, 1:2], func=AF.Sqrt, bias=epsb[0:rows, :], scale=1.0),
                 reads=[mv, small], writes=[mv])
            P.op("dve", lambda e: e.reciprocal(out=mv[0:rows, 3:4], in_=mv[0:rows, 2:3]), reads=[mv], writes=[mv])

        def ln_transpose(ti, rows, col0, gsh, gsc, sample):
            ln_stats(R, ti, rows)
            P.op("dve", lambda e: e.tensor_scalar(out=xn[0:rows, :], in0=R[0:rows, ti, :], scalar1=mv[0:rows, 0:1],
                                                  scalar2=mv[0:rows, 3:4], op0=ALU.subtract, op1=ALU.mult),
                 reads=[Rt[ti], mv], writes=[xn])
            for half in range(2):
                pt = nxt("t", PT)
                for kk in range(8):
                    k = half * 8 + kk
                    P.op("pe", lambda e, k=k, kk=kk, pt=pt: e.transpose(out=pt[:, kk, 0:rows], in_=xn[0:rows, k * 128:(k + 1) * 128],
                                                                        identity=identb[0:rows, 0:rows]),
                         reads=[xn, identb], writes=[pt], signal=(kk == 7))
                for kk in range(8):
                    k = half * 8 + kk
                    if not sample:
                        if half == 0:
                            P.op("dve", lambda e, k=k, kk=kk, pt=pt: e.tensor_scalar(
                                out=uT[:, k, col0:col0 + rows], in0=pt[:, kk, 0:rows], scalar1=modT[:, gsc, k, 0:1],
                                scalar2=modT[:, gsh, k, 0:1], op0=ALU.mult, op1=ALU.add), reads=[pt, modT], writes=[uT])
                        else:
                            P.op("act", lambda e, k=k, kk=kk, pt=pt: e.activation(
                                out=uT[:, k, col0:col0 + rows], in_=pt[:, kk, 0:rows], func=AF.Identity,
                                bias=modT[:, gsh, k, 0:1], scale=modT[:, gsc, k, 0:1]), reads=[pt, modT], writes=[uT])
                    else:
                        et = nxt("e", etmp)
                        P.op("dve", lambda e, k=k, kk=kk, pt=pt, et=et: e.tensor_tensor(
                            out=et[:, 0:TS].rearrange("p (b t) -> p b t", t=4),
                            in0=pt[:, kk, 0:TS].rearrange("p (b t) -> p b t", t=4),
                            in1=modT[:, gsc, k, 1:17].unsqueeze(2).to_broadcast([128, 16, 4]), op=ALU.mult),
                            reads=[pt, modT], writes=[et])
                        P.op("dve", lambda e, k=k, et=et: e.tensor_tensor(
                            out=uT[:, k, col0:col0 + TS].rearrange("p (b t) -> p b t", t=4),
                            in0=et[:, 0:TS].rearrange("p (b t) -> p b t", t=4),
                            in1=modT[:, gsh, k, 1:17].unsqueeze(2).to_broadcast([128, 16, 4]), op=ALU.add),
                            reads=[et, modT], writes=[uT])

        def gate_math(ti, rows, sample, masked=False):
            Fb, gb, Mb, ib, wcb, dcb, enb, mtb = [gs[n] for n in ["F", "g", "M", "inter", "wc", "dcr", "en", "mtok"]]
            r = slice(0, rows)
            ea, lg = sc[r, 0:4], sc[r, 4:8]
            P.op("act", lambda e: e.activation(out=ea, in_=GT[r, ti, 4:8], func=AF.Exp, scale=-1.0), reads=[GT], writes=[sc])
            P.op("act", lambda e: e.activation(out=lg, in_=ea, func=AF.Ln, bias=1.0), reads=[sc], writes=[sc])
            if masked:
                P.op("dve", lambda e: e.tensor_scalar(out=lg, in0=lg, scalar1=small[r, S_PM:S_PM + 1], scalar2=None, op0=ALU.mult),
                     reads=[sc, small], writes=[sc])
            ps = nxt("a", PA)
            tri = con(C_STRI, TS, TS) if sample else con(C_TRI, 128)
            mm(ps, ps[r, 0:4], [(tri, lg)], [CON, sc])
            if sample:
                P.op("dve", lambda e: e.tensor_scalar(out=Fb[r, ti, :], in0=ps[r, 0:4], scalar1=-1.0, scalar2=None, op0=ALU.mult),
                     reads=[ps], writes=[Fb])
            else:
                ps2 = nxt("a", PA)
                mm(ps2, ps2[:, 0:4], [(con(C_ONE, 128), lg)], [CON, sc])
                P.op("dve", lambda e: e.scalar_tensor_tensor(out=Fb[:, ti, :], in0=ps[:, 0:4], scalar=-1.0, in1=FM[:, 0:4],
                                                             op0=ALU.mult, op1=ALU.add), reads=[ps, FM], writes=[Fb])
                P.op("dve", lambda e: e.scalar_tensor_tensor(out=FM[:, 0:4], in0=ps2[:, 0:4], scalar=-1.0, in1=FM[:, 0:4],
                                                             op0=ALU.mult, op1=ALU.add), reads=[ps2, FM], writes=[FM])
            P.op("dve", lambda e: e.tensor_tensor(out=gb[r, ti, :], in0=GT[r, ti, 0:4], in1=Fb[r, ti, :], op=ALU.subtract),
                 reads=[GT, Fb], writes=[gb])
            if masked:
                P.op("dve", lambda e: e.tensor_scalar(out=gb[r, ti, :], in0=gb[r, ti, :], scalar1=small[r, S_PM:S_PM + 1],
                                                      scalar2=small[r, S_PMOFF:S_PMOFF + 1], op0=ALU.mult, op1=ALU.add),
                     reads=[gb, small], writes=[gb])
            ident = con(C_ID, rows, rows)
            P.op("dve", lambda e: e.tensor_tensor(out=diag4[r, :, 0:rows], in0=ident.unsqueeze(1).to_broadcast([rows, 4, rows]),
                                                  in1=gb[r, ti, :].unsqueeze(2).to_broadcast([rows, 4, rows]), op=ALU.mult),
                 reads=[CON, gb], writes=[diag4])
            pg = nxt("a", PA)
            pgv = pg[:, :].rearrange("p (h s) -> p h s", h=4)
            for h in range(4):
                P.op("pe", lambda e, h=h: e.matmul(pgv[:, h, 0:rows], lhsT=CON[r, C_ONE:C_ONE + 128], rhs=diag4[r, h, 0:rows],
                                                   start=True, stop=True), reads=[CON, diag4], writes=[pg], signal=(h == 3))
            negc = con(C_SNEGC, TS, TS) if sample else con(C_NEGC, 128)
            P.op("dve", lambda e: e.tensor_tensor(out=tmp4[r, :, 0:rows], in0=pgv[r, :, 0:rows],
                                                  in1=negc.unsqueeze(1).to_broadcast([rows, 4, rows]), op=ALU.add),
                 reads=[pg, CON], writes=[tmp4])
            mloc = sc[r, 8:12]
            P.op("dve", lambda e: e.tensor_reduce(out=mloc, in_=tmp4[r, :, 0:rows], axis=AX.X, op=ALU.max), reads=[tmp4], writes=[sc])
            if sample:
                m0t = small[r, S_M0T:S_M0T + 4]
                mcur = m0t
                P.op("dve", lambda e: e.tensor_tensor(out=tmp4[r, :, 0:rows], in0=pgv[r, :, 0:rows],
                                                      in1=con(C_SNEGB, TS, TS).unsqueeze(1).to_broadcast([rows, 4, rows]), op=ALU.add),
                     reads=[pg, CON, sc], writes=[tmp4])
                mend = sc[r, 12:16]
                P.op("dve", lambda e: e.tensor_reduce(out=mend, in_=tmp4[r, :, 0:rows], axis=AX.X, op=ALU.max), reads=[tmp4], writes=[sc])
                P.op("dve", lambda e: e.tensor_tensor(out=mend, in0=mend, in1=m0t, op=ALU.max), reads=[sc, small], writes=[sc])
                mrep = sc[:, 32:48]
                for h in range(4):
                    P.op("dve", lambda e, h=h: e.tensor_reduce(out=sc[:, 48:64], in_=pgv[:, h, 0:TS].rearrange("p (b t) -> p b t", t=4),
                                                               axis=AX.X, op=ALU.max), reads=[pg], writes=[sc])
                    m0r = small[:, S_M0R + h * 16:S_M0R + (h + 1) * 16]
                    P.op("dve", lambda e, m0r=m0r: e.tensor_tensor(out=sc[:, 48:64], in0=sc[:, 48:64], in1=m0r, op=ALU.max),
                         reads=[sc, small], writes=[sc])
                    P.op("dve", lambda e, m0r=m0r: e.tensor_tensor(out=sc[:, 48:64], in0=m0r, in1=sc[:, 48:64], op=ALU.subtract),
                         reads=[sc, small], writes=[sc])
                    P.op("act", lambda e, h=h: e.activation(out=dcrs[:, h, :], in_=sc[:, 48:64], func=AF.Exp), reads=[sc], writes=[dcrs])
            else:
                mcur = FM[:, 4:8]
                tmax = sc[:, 12:16]
                P.op("dve", lambda e: e.tensor_reduce(out=tmax, in_=pgv[:, :, :], axis=AX.X, op=ALU.max), reads=[pg], writes=[sc])
                mend = FM[:, 12:16]
                P.op("dve", lambda e: e.tensor_tensor(out=mend, in0=mcur, in1=tmax, op=ALU.max), reads=[FM, sc], writes=[FM])
            P.op("dve", lambda e: e.tensor_tensor(out=Mb[r, ti, :], in0=mloc, in1=mcur, op=ALU.max), reads=[sc, FM, small], writes=[Mb])
            d1, d2, d3, d4 = sc[r, 16:20], sc[r, 20:24], sc[r, 24:28], sc[r, 28:32]
            P.op("dve", lambda e: e.tensor_tensor(out=d1, in0=mcur, in1=Mb[r, ti, :], op=ALU.subtract), reads=[FM, small, Mb], writes=[sc])
            P.op("act", lambda e: e.activation(out=ib[r, ti, :], in_=d1, func=AF.Exp), reads=[sc], writes=[ib])
            P.op("dve", lambda e: e.tensor_tensor(out=d2, in0=gb[r, ti, :], in1=mend, op=ALU.subtract), reads=[gb, FM, sc], writes=[sc])
            P.op("act", lambda e: e.activation(out=wcb[r, ti, :], in_=d2, func=AF.Exp), reads=[sc], writes=[wcb])
            if not sample:
                P.op("dve", lambda e: e.tensor_tensor(out=d3, in0=mcur, in1=mend, op=ALU.subtract), reads=[FM], writes=[sc])
                P.op("act", lambda e: e.activation(out=dcb[:, ti, :], in_=d3, func=AF.Exp), reads=[sc], writes=[dcb])
            P.op("dve", lambda e: e.tensor_tensor(out=mtb[r, ti, :], in0=Fb[r, ti, :], in1=Mb[r, ti, :], op=ALU.add), reads=[Fb, Mb], writes=[mtb])
            P.op("act", lambda e: e.activation(out=enb[r, ti, :], in_=mtb[r, ti, :], func=AF.Exp, scale=-1.0), reads=[mtb], writes=[enb])
            P.op("dve", lambda e: e.tensor_tensor(out=diag4[r, :, 0:rows], in0=ident.unsqueeze(1).to_broadcast([rows, 4, rows]),
                                                  in1=Mb[r, ti, :].unsqueeze(2).to_broadcast([rows, 4, rows]), op=ALU.mult),
                 reads=[CON, Mb, pg], writes=[diag4])
            pm_ = nxt("a", PA)
            pmv = pm_[:, :].rearrange("p (h s) -> p h s", h=4)
            for h in range(4):
                P.op("pe", lambda e, h=h: e.matmul(pmv[r, h, 0:rows], lhsT=CON[r, C_ONE:C_ONE + rows], rhs=diag4[r, h, 0:rows],
                                                   start=True, stop=True), reads=[CON, diag4], writes=[pm_], signal=(h == 3))
            negct = con(C_SNEGCT, TS, TS) if sample else con(C_NEGCT, 128)
            P.op("dve", lambda e: e.scalar_tensor_tensor(out=tmp4[r, :, 0:rows], in0=pmv[r, :, 0:rows], scalar=-1.0,
                                                         in1=negct.unsqueeze(1).to_broadcast([rows, 4, rows]), op0=ALU.mult, op1=ALU.add),
                 reads=[pm_, CON], writes=[tmp4])
            for h in range(4):
                P.op("act", lambda e, h=h: e.activation(out=DT[r, ti, h, 0:rows], in_=tmp4[r, h, 0:rows], func=AF.Exp,
                                                        bias=gb[r, ti, h:h + 1], scale=1.0), reads=[tmp4, gb], writes=[DT])
            if not sample:
                P.op("dve", lambda e: e.tensor_copy(out=FM[:, 4:8], in_=FM[:, 12:16]), reads=[FM, sc], writes=[FM])

        def finalize(h, ti, rows, col0, en_ap):
            r = slice(0, rows)
            a = sc[r, 0:1]
            P.op("dve", lambda e: e.scalar_tensor_tensor(out=a, in0=numx[r, 256:257], scalar=-1.0, in1=numx[r, 256:257],
                                                         op0=ALU.mult, op1=ALU.max), reads=[numx], writes=[sc])
            P.op("dve", lambda e: e.tensor_tensor(out=a, in0=a, in1=en_ap, op=ALU.max), reads=[sc, gs["en"]], writes=[sc])
            P.op("dve", lambda e: e.reciprocal(out=sc[r, 1:2], in_=a), reads=[sc], writes=[sc])
            P.op("dve", lambda e: e.bn_stats(out=stt[r, 0, :], in_=numx[r, 0:256]), reads=[numx], writes=[stt])
            P.op("dve", lambda e: e.bn_aggr(out=mv[r, 0:2], in_=stt[r, 0, :]), reads=[stt], writes=[mv])
            P.op("dve", lambda e: e.tensor_tensor(out=sc[r, 2:3], in0=sc[r, 1:2], in1=sc[r, 1:2], op=ALU.mult), reads=[sc], writes=[sc])
            P.op("dve", lambda e: e.tensor_tensor(out=sc[r, 2:3], in0=sc[r, 2:3], in1=mv[r, 1:2], op=ALU.mult), reads=[sc, mv], writes=[sc])
            P.op("act", lambda e: e.activation(out=sc[r, 3:4], in_=sc[r, 2:3], func=AF.Sqrt, bias=epsb[r, :], scale=1.0),
                 reads=[sc, small], writes=[sc])
            P.op("dve", lambda e: e.reciprocal(out=sc[r, 4:5], in_=sc[r, 3:4]), reads=[sc], writes=[sc])
            P.op("dve", lambda e: e.tensor_tensor(out=sc[r, 4:5], in0=sc[r, 4:5], in1=sc[r, 1:2], op=ALU.mult), reads=[sc], writes=[sc])
            P.op("dve", lambda e: e.tensor_scalar(out=hn[r, :], in0=numx[r, 0:256], scalar1=mv[r, 0:1], scalar2=sc[r, 4:5],
                                                  op0=ALU.subtract, op1=ALU.mult), reads=[numx, mv, sc], writes=[hn])
            pt = nxt("t", PT)
            for c in range(2):
                P.op("pe", lambda e, c=c: e.transpose(out=pt[:, c, 0:rows], in_=hn[r, c * 128:(c + 1) * 128], identity=identb[r, r]),
                     reads=[hn, identb], writes=[pt], signal=(c == 1))
            for c in range(2):
                sg = sigs[:, 2 * h + c, :] if rows == TS else sigo[:, c, col0:col0 + rows]
                P.op("dve", lambda e, c=c, sg=sg: e.scalar_tensor_tensor(
                    out=yT[:, 8 + 2 * h + c, col0:col0 + rows], in0=pt[:, c, 0:rows],
                    scalar=small[:, S_WMH + 2 * h + c:S_WMH + 2 * h + c + 1], in1=sg, op0=ALU.mult, op1=ALU.mult),
                    reads=[pt, small, sigo, sigs], writes=[yT])

        def proj_resid(w_r, nk, lhs_buf, tiles, Gp, Gs):
            for nb in range(4):
                groups = [(k0, min(k0 + KC, nk)) for k0 in range(0, nk, KC)]
                acc = {}
                for gi_, (k0, k1) in enumerate(groups):
                    wb = get_unit([(0, w_r[:, k0:k1, nb * 512:(nb + 1) * 512])])
                    for (ti, rows, col0, smp) in tiles:
                        if gi_ == 0:
                            acc[ti] = PA[ti] if len(groups) > 1 else nxt("a", PA)
                        ps = acc[ti]
                        for k in range(k0, k1):
                            P.op("pe", lambda e, ps=ps, k=k, k0=k0, wb=wb, rows=rows, col0=col0: e.matmul(
                                ps[0:rows, :], lhsT=lhs_buf[:, k, col0:col0 + rows], rhs=wb[:, k - k0, :],
                                start=(k == 0), stop=(k == nk - 1)), reads=[lhs_buf, wb], writes=[ps], signal=(k == k1 - 1))
                for (ti, rows, col0, smp) in tiles:
                    ps = acc[ti]
                    Gb = Gs if smp else Gp
                    et = nxt("e", etmp)
                    P.op("dve", lambda e, ps=ps, rows=rows, Gb=Gb, et=et, nb=nb: e.tensor_tensor(
                        out=et[0:rows, :], in0=ps[0:rows, :], in1=Gb[0:rows, nb * 512:(nb + 1) * 512], op=ALU.mult),
                        reads=[ps, Gb], writes=[et])
                    P.op("dve", lambda e, ti=ti, rows=rows, et=et, nb=nb: e.scalar_tensor_tensor(
                        out=R[0:rows, ti, nb * 512:(nb + 1) * 512], in0=R[0:rows, ti, nb * 512:(nb + 1) * 512], scalar=ALPHA,
                        in1=et[0:rows, :], op0=ALU.mult, op1=ALU.add), reads=[Rt[ti], et], writes=[Rt[ti]])

        def ln_affine(ti, rows, li):
            ln_stats(R, ti, rows)
            P.op("dve", lambda e: e.tensor_scalar(out=R[0:rows, ti, :], in0=R[0:rows, ti, :], scalar1=mv[0:rows, 0:1],
                                                  scalar2=mv[0:rows, 3:4], op0=ALU.subtract, op1=ALU.mult),
                 reads=[Rt[ti], mv], writes=[Rt[ti]])
            P.op("dve", lambda e: e.tensor_tensor(out=R[0:rows, ti, :], in0=R[0:rows, ti, :], in1=uTf[0:rows, 0, :], op=ALU.mult),
                 reads=[Rt[ti], uT], writes=[Rt[ti]])
            P.op("pool", lambda e: e.tensor_tensor(out=R[0:rows, ti, :], in0=R[0:rows, ti, :], in1=uTf[0:rows, 1, :], op=ALU.add),
                 reads=[Rt[ti], uT], writes=[Rt[ti]])

        def ln_prefetch(li):
            P.dma("act", uTf[:, 0, :], ln_d[2 * li], writes=[uT])
            P.dma("act", uTf[:, 1, :], ln_d[2 * li + 1], writes=[uT], nodeps=True)

        wcv = lambda j, i: small[:, S_WCV + j * 3 + i:S_WCV + j * 3 + i + 1]
        wfc = lambda j, i: small[:, S_WFC + j * 3 + i:S_WFC + j * 3 + i + 1]

        def conv3(out_ap, src, wfn, j, n, tmp_ap):
            P.op("dve", lambda e: e.tensor_scalar(out=tmp_ap, in0=src(0), scalar1=wfn(j, 0), scalar2=None, op0=ALU.mult),
                 reads=[prod, abuf, psx, asx, small], writes=[ctmp, etmp[0], etmp[1]])
            P.op("dve", lambda e: e.scalar_tensor_tensor(out=tmp_ap, in0=src(1), scalar=wfn(j, 1), in1=tmp_ap, op0=ALU.mult, op1=ALU.add),
                 reads=[prod, abuf, psx, asx, small], writes=[ctmp, etmp[0], etmp[1]])
            P.op("dve", lambda e: e.scalar_tensor_tensor(out=out_ap, in0=src(2), scalar=wfn(j, 2), in1=tmp_ap, op0=ALU.mult, op1=ALU.add),
                 reads=[prod, abuf, psx, asx, small], writes=[ctmp, etmp[0], etmp[1]])

        _ck(P, 'ada')
        kws = P.sbuf("kws", [TS, 4, 256], BF16)
        vexts = P.sbuf("vexts", [TS, 4, 257], BF16)
        print("SBUF used per partition (final):", P.cur, "of", P.limit)
        first = True
        first_pre = True
        first_main = True
        passes = [("pre", 0, 4), ("pre", 512, 3), ("main", 896, 4), ("mains", 1408, 3), ("main", 1792, 2)]
        for pi, (kind, tok0, nt) in enumerate(passes):
            p = "%s%d" % (kind, pi)
            has_s = (kind == "mains")
            pre = (kind == "pre")
            halo = (kind == "main" and tok0 == 896)
            tiles = [(ti, 128, ti * 128, False) for ti in range(nt)]
            blocks = [(0, nt * 128, False)]
            s_ti, s_col = nt, nt * 128
            if has_s:
                tiles = tiles + [(s_ti, TS, s_col, True)]
                blocks = blocks + [(s_col, TS, True)]
            if not first:
                P.handoff(ffn_bufs, mixer_bufs)
            first = False
            if not pre and first_main:
                for _ in ada_gen:
                    pass
                P.handoff([sT], [yT])
            if pre:
                ust["u"] = 58
                ust["first"] = first_pre
                first_pre = False
            else:
                ust["u"] = 0
                ust["first"] = first_main
                first_main = False
            _ck(P, 'A_' + str(p))
            for (ti, rows, col0, smp) in tiles:
                src = xs if smp else xp[tok0 + ti * 128:tok0 + (ti + 1) * 128, :]
                P.dma("act", R[0:rows, ti, :], src, writes=[Rt[ti]])
                ln_transpose(ti, rows, col0, 0, 1, smp)
            _ck(P, 'B_' + str(p))
            P.op("pool", lambda e: e.memset(vext[:, :, :, 256:257], 1.0), writes=[vext])
            for (ti, rows, col0, smp) in tiles:
                ps = nxt("a", PA)
                mm(ps, ps[0:rows, 0:8], [(uT[:, k, col0:col0 + rows], wg[:, k, :]) for k in range(KC)], [uT, wg])
                P.op("dve", lambda e, ps=ps, ti=ti, rows=rows: e.tensor_tensor(out=GT[0:rows, ti, :], in0=ps[0:rows, 0:8],
                                                                              in1=small[0:rows, S_BG:S_BG + 8], op=ALU.add),
                     reads=[ps, small], writes=[GT])
                gate_math(ti, rows, smp, masked=(pre or (halo and ti == 0)))
            _ck(P, 'C_' + str(p))
            for h in range(NH):
                wb = get_unit([(0, w_in_r[:, :, 4096 + h * 256:4096 + (h + 1) * 256]), (256, w_in_r[:, :, 5120 + h * 256:5120 + (h + 1) * 256])])
                for (ti, rows, col0, smp) in tiles:
                    ps = nxt("a", PA)
                    mm(ps, ps[0:rows, :], [(uT[:, k, col0:col0 + rows], wb[:, k, :]) for k in range(KC)], [uT, wb])
                    P.op("dve", lambda e, ps=ps, ti=ti, rows=rows, h=h: e.tensor_scalar(
                        out=kw[0:rows, 0, ti, :], in0=ps[0:rows, 0:256], scalar1=gs["wc"][0:rows, ti, h:h + 1], scalar2=1.0 / 16.0,
                        op0=ALU.mult, op1=ALU.mult), reads=[ps, gs["wc"]], writes=[kw])
                    P.op("act", lambda e, ps=ps, ti=ti, rows=rows: e.copy(out=vext[0:rows, 0, ti, 0:256], in_=ps[0:rows, 256:512]),
                         reads=[ps], writes=[vext])
                if pre:
                    for (ti, rows, col0, smp) in tiles:
                        for c in range(2):
                            ps_u = nxt("a", PA)
                            mm(ps_u, ps_u[:, 0:257], [(kw[:, 0, ti, c * 128:(c + 1) * 128], vext[:, 0, ti, :])], [kw, vext])
                            P.op("dve", lambda e, ps_u=ps_u, c=c, ti=ti, h=h: e.scalar_tensor_tensor(
                                out=Cst[:, h, c, :], in0=Cst[:, h, c, :], scalar=gs["dcr"][:, ti, h:h + 1], in1=ps_u[:, 0:257],
                                op0=ALU.mult, op1=ALU.add), reads=[Cst, gs["dcr"], ps_u], writes=[Cst])
                        next(ada_gen, None)
                    P.op("act", lambda e, h=h: e.copy(out=Cbf[:, h, :, :], in_=Cst[:, h, :, :]), reads=[Cst], writes=[Cbf])
                    continue
                _ck(P, 'Ckv_%s_%d' % (p, h))
                wb = get_unit([(0, w_in_r[:, :, 3072 + h * 256:3072 + (h + 1) * 256]), (256, w_in_r[:, :, 4096 + h * 256:4096 + (h + 1) * 256])])
                wb2 = get_unit([(0, w_in_r[:, :, 6144 + h * 256:6144 + (h + 1) * 256])])
                for cc in range(6):
                    wsel = wb if cc < 4 else wb2
                    wc0 = (cc % 4) * 128
                    for (b0, bn, smp) in blocks:
                        ps = nxt("a", PA)
                        mm(ps, ps[:, 0:bn], [(wsel[:, k, wc0:wc0 + 128], uT[:, k, b0:b0 + bn]) for k in range(KC)], [uT, wsel])
                        if cc < 2:
                            P.op("act", lambda e, ps=ps, cc=cc, b0=b0, bn=bn: e.copy(out=qT[:, cc, b0:b0 + bn], in_=ps[:, 0:bn]),
                                 reads=[ps], writes=[qT])
                        elif cc < 4:
                            P.op("act", lambda e, ps=ps, cc=cc, b0=b0, bn=bn: e.mul(out=kT[:, cc - 2, b0:b0 + bn], in_=ps[:, 0:bn], mul=1.0 / 16.0),
                                 reads=[ps], writes=[kT])
                        else:
                            P.op("act", lambda e, ps=ps, cc=cc, b0=b0, bn=bn: e.activation(out=sigo[:, cc - 4, b0:b0 + bn], in_=ps[:, 0:bn], func=AF.Sigmoid),
                                 reads=[ps], writes=[sigo])
                _ck(P, 'Cfm_%s_%d' % (p, h))
                for (ti, rows, col0, smp) in tiles:
                    r = slice(0, rows)
                    ps_s = nxt("a", PA)
                    mm(ps_s, ps_s[r, 0:rows], [(kT[:, c, col0:col0 + rows], qT[:, c, col0:col0 + rows]) for c in range(2)], [kT, qT])
                    P.op("dve", lambda e, ps_s=ps_s, r=r, rows=rows, ti=ti, h=h: e.tensor_tensor(
                        out=SwT[r, 0:rows], in0=ps_s[r, 0:rows], in1=DT[r, ti, h, 0:rows], op=ALU.mult), reads=[ps_s, DT], writes=[SwT])
                    ps_a = nxt("a", PA)
                    mm(ps_a, ps_a[r, 0:257], [(SwT[r, 0:rows], vext[r, 0, ti, :])], [SwT, vext])
                    if smp:
                        P.op("act", lambda e, ps_a=ps_a, h=h: e.copy(out=As_sb[:, h, :], in_=ps_a[0:TS, 0:257]), reads=[ps_a], writes=[As_sb])
                        P.op("act", lambda e, h=h, col0=col0: e.copy(out=qTs[:, 2 * h:2 * h + 2, :], in_=qT[:, :, col0:col0 + TS]), reads=[qT], writes=[qTs])
                        P.op("act", lambda e, h=h, col0=col0: e.copy(out=sigs[:, 2 * h:2 * h + 2, :], in_=sigo[:, :, col0:col0 + TS]), reads=[sigo], writes=[sigs])
                        P.op("act", lambda e, h=h, ti=ti: e.copy(out=kws[:, h, :], in_=kw[0:TS, 0, ti, :]), reads=[kw], writes=[kws])
                        P.op("act", lambda e, h=h, ti=ti: e.copy(out=vexts[:, h, :], in_=vext[0:TS, 0, ti, :]), reads=[vext], writes=[vexts])
                        continue
                    ps_b = nxt("a", PA)
                    mm(ps_b, ps_b[:, 0:257], [(qT[:, c, col0:col0 + 128], Cbf[:, h, c, :]) for c in range(2)], [qT, Cbf])
                    P.op("act", lambda e, ps_b=ps_b, ti=ti, h=h: e.activation(out=Bs[:, :], in_=ps_b[:, 0:257], func=AF.Copy,
                                                                              scale=gs["inter"][:, ti, h:h + 1]), reads=[ps_b, gs["inter"]], writes=[Bs])
                    P.op("dve", lambda e, ps_a=ps_a: e.tensor_tensor(out=numx[:, :], in0=ps_a[:, 0:257], in1=Bs[:, :], op=ALU.add),
                         reads=[ps_a, Bs], writes=[numx])
                    finalize(h, ti, 128, col0, gs["en"][:, ti, h:h + 1])
                    for c in range(2):
                        ps_u = nxt("a", PA)
                        mm(ps_u, ps_u[:, 0:257], [(kw[:, 0, ti, c * 128:(c + 1) * 128], vext[:, 0, ti, :])], [kw, vext])
                        P.op("dve", lambda e, ps_u=ps_u, c=c, ti=ti, h=h: e.scalar_tensor_tensor(
                            out=Cst[:, h, c, :], in0=Cst[:, h, c, :], scalar=gs["dcr"][:, ti, h:h + 1], in1=ps_u[:, 0:257],
                            op0=ALU.mult, op1=ALU.add), reads=[Cst, gs["dcr"], ps_u], writes=[Cst])
                    P.op("act", lambda e, h=h: e.copy(out=Cbf[:, h, :, :], in_=Cst[:, h, :, :]), reads=[Cst], writes=[Cbf])
            if pre:
                continue
            _ck(P, 'C2_' + str(p))
            if has_s:
                P.handoff(head_bufs + conv_bufs, samp_bufs)
                accb = [PA[0], PA[1], PA[2], PA[3]]
                for b in range(16):
                    Qm = QmR[b % 2]
                    P.op("dve", lambda e, b=b, Qm=Qm: e.tensor_tensor(out=Qm[:, :, :], in0=qTs[:, :, :],
                                                                      in1=CON[:, C_BMREP + b * 64:C_BMREP + (b + 1) * 64].unsqueeze(1).to_broadcast([128, 8, TS]),
                                                                      op=ALU.mult), reads=[qTs, CON], writes=[Qm])
                    for hf in range(2):
                        Cf, Cb = C0fH[hf], C0bH[hf]
                        P.dma("sp", Cf[:, :, :, 0:256], C0_d[b, 2 * hf:2 * hf + 2].rearrange("h (c q) e -> q h c e", q=128), writes=[Cf])
                        P.op("dve", lambda e, b=b, hf=hf, Cf=Cf: e.tensor_copy(out=Cf[:, :, :, 256:257], in_=n0T[:, b, 2 * hf:2 * hf + 2, :].unsqueeze(3)),
                             reads=[n0T], writes=[Cf])
                        P.op("act", lambda e, Cf=Cf, Cb=Cb: e.copy(out=Cb[:, :, :, :], in_=Cf[:, :, :, :]), reads=[Cf], writes=[Cb])
                        for hl in range(2):
                            h = 2 * hf + hl
                            ps = accb[h]
                            for c in range(2):
                                P.op("pe", lambda e, ps=ps, h=h, hl=hl, c=c, b=b, Qm=Qm, Cb=Cb: e.matmul(
                                    ps[0:TS, 0:257], lhsT=Qm[:, 2 * h + c, :], rhs=Cb[:, hl, c, :],
                                    start=(b == 0 and c == 0), stop=(b == 15 and c == 1)),
                                    reads=[Qm, Cb], writes=[ps], signal=(c == 1))
                        for hl in range(2):
                            h = 2 * hf + hl
                            kwm = kwmR[hl]
                            P.op("dve", lambda e, h=h, b=b, kwm=kwm: e.tensor_scalar(out=kwm[:, :], in0=kws[:, h, :], scalar1=CON[0:TS, C_BM + b:C_BM + b + 1],
                                                                                    scalar2=None, op0=ALU.mult), reads=[kws, CON], writes=[kwm])
                            for c in range(2):
                                ps_u = PA[4 + c]
                                mm(ps_u, ps_u[:, 0:257], [(kwm[:, c * 128:(c + 1) * 128], vexts[:, h, :])], [kwm, vexts])
                                P.op("dve", lambda e, ps_u=ps_u, c=c, h=h, hl=hl, b=b, Cf=Cf: e.scalar_tensor_tensor(
                                    out=Cf[:, hl, c, :], in0=Cf[:, hl, c, :], scalar=dcrs[:, h, b:b + 1], in1=ps_u[:, 0:257],
                                    op0=ALU.mult, op1=ALU.add), reads=[Cf, dcrs, ps_u, Cb], writes=[Cf])
                        P.op("act", lambda e, b=b, hf=hf, Cf=Cf: e.copy(out=sn_o[:, b, 2 * hf:2 * hf + 2, :], in_=Cf[:, :, :, 256]), reads=[Cf], writes=[sn_o])
                        P.dma("act", o_sC[b, 2 * hf:2 * hf + 2].rearrange("h (c q) e -> q h c e", q=128), Cf[:, :, :, 0:256], reads=[Cf], is_out=True)
                for h in range(NH):
                    P.op("act", lambda e, h=h, s_ti=s_ti, accb=accb: e.activation(out=Bs[0:TS, :], in_=accb[h][0:TS, 0:257], func=AF.Copy,
                                                            scale=gs["inter"][0:TS, s_ti, h:h + 1]), reads=[accb[h], gs["inter"]], writes=[Bs])
                    P.op("dve", lambda e, h=h: e.tensor_tensor(out=numx[0:TS, :], in0=As_sb[:, h, :], in1=Bs[0:TS, :], op=ALU.add),
                         reads=[As_sb, Bs], writes=[numx])
                    finalize(h, s_ti, TS, s_col, gs["en"][0:TS, s_ti, h:h + 1])
                P.dma("sp", o_sn, sn_o[:, :, :, :].rearrange("p a b c -> p (a b c)"), reads=[sn_o], is_out=True)
                P.dma("sp", o_sm, gs["mtok"][0:TS, s_ti, :], reads=[gs["mtok"]], is_out=True)
                P.handoff(samp_bufs, head_bufs + conv_bufs)
            _ck(P, 'D_' + str(p))
            for j in range(8):
                wb = get_unit([(0, w_in_r[:, :, j * 128:(j + 1) * 128]), (128, w_in_r[:, :, 1024 + j * 128:1024 + (j + 1) * 128]),
                               (256, w_in_r[:, :, 2048 + j * 128:2048 + (j + 1) * 128])])
                for (b0, bn, smp) in blocks:
                    pss = []
                    for q in range(3):
                        ps = nxt("a", PA)
                        mm(ps, ps[:, 0:bn], [(wb[:, k, q * 128:(q + 1) * 128], uT[:, k, b0:b0 + bn]) for k in range(KC)], [uT, wb])
                        pss.append(ps)
                    pB, pC, pH = pss
                    P.op("act", lambda e, pC=pC, bn=bn: e.copy(out=Csb[:, 0:bn], in_=pC[:, 0:bn]), reads=[pC], writes=[Csb])
                    if not smp:
                        P.op("pool", lambda e, j=j: e.tensor_copy(out=prod[:, 0:2], in_=prodH[:, j, :]), reads=[prodH], writes=[prod])
                        P.op("dve", lambda e, pH=pH, bn=bn: e.tensor_tensor(out=prod[:, 2:2 + bn], in0=Csb[:, 0:bn], in1=pH[:, 0:bn], op=ALU.mult),
                             reads=[Csb, pH], writes=[prod])
                        if halo:
                            P.op("dve", lambda e: e.tensor_scalar(out=prod[:, 2:130], in0=prod[:, 2:130], scalar1=small[:, S_PM:S_PM + 1],
                                                                  scalar2=None, op0=ALU.mult), reads=[prod, small], writes=[prod])
                        conv3(ctmp[:, 0:bn], lambda o, bn=bn: prod[:, o:o + bn], wcv, j, bn, ctmp[:, 0:bn])
                        P.op("dve", lambda e, pB=pB, j=j, bn=bn: e.tensor_tensor(out=yT[:, j, 0:bn], in0=pB[:, 0:bn], in1=ctmp[:, 0:bn], op=ALU.mult),
                             reads=[pB, ctmp], writes=[yT])
                        P.op("pool", lambda e, j=j, bn=bn: e.tensor_copy(out=prodH[:, j, :], in_=prod[:, bn:bn + 2]), reads=[prod], writes=[prodH])
                    else:
                        P.op("pool", lambda e, j=j: e.tensor_copy(out=psx[:, :, 0:2], in_=sconvT[:, j, :, :]), reads=[sconvT], writes=[psx])
                        P.op("dve", lambda e, pH=pH: e.tensor_tensor(out=psx[:, :, 2:6], in0=Csb[:, 0:TS].rearrange("p (b t) -> p b t", t=4),
                                                                     in1=pH[:, 0:TS].rearrange("p (b t) -> p b t", t=4), op=ALU.mult),
                             reads=[Csb, pH], writes=[psx])
                        cv = ctmp[:, 0:TS].rearrange("p (b t) -> p b t", t=4)
                        conv3(cv, lambda o: psx[:, :, o:o + 4], wcv, j, 4, cv)
                        P.op("dve", lambda e, pB=pB, j=j, cv=cv, b0=b0: e.tensor_tensor(out=yT[:, j, b0:b0 + TS].rearrange("p (b t) -> p b t", t=4),
                                                                                in0=pB[:, 0:TS].rearrange("p (b t) -> p b t", t=4), in1=cv, op=ALU.mult),
                             reads=[pB, ctmp], writes=[yT])
                        P.op("pool", lambda e, j=j: e.tensor_copy(out=sconv_o[:, j, :, :], in_=psx[:, :, 4:6]), reads=[psx], writes=[sconv_o])
            _ck(P, 'E_' + str(p))
            ln_prefetch(0)
            proj_resid(w_out_r, KC, yT, tiles, G[0], G[1])
            for (ti, rows, col0, smp) in tiles:
                ln_affine(ti, rows, 0)
            _ck(P, 'F1_' + str(p))
            for (ti, rows, col0, smp) in tiles:
                ln_transpose(ti, rows, col0, 2, 3, smp)
            _ck(P, 'F2_' + str(p))
            P.handoff(mixer_bufs, ffn_bufs)
            for jj in range(JF // 2):
                wb = get_unit([(0, w_up_r[:, :, jj * 256:(jj + 1) * 256]), (256, w_up_r[:, :, DFF + jj * 256:DFF + (jj + 1) * 256])])
                for cj in range(2):
                    j = jj * 2 + cj
                    for (b0, bn, smp) in blocks:
                        pa = nxt("a", PA)
                        mm(pa, pa[:, 0:bn], [(wb[:, k, cj * 128:(cj + 1) * 128], uT[:, k, b0:b0 + bn]) for k in range(KC)], [uT, wb])
                        pg_ = nxt("a", PA)
                        mm(pg_, pg_[:, 0:bn], [(wb[:, k, 256 + cj * 128:256 + (cj + 1) * 128], uT[:, k, b0:b0 + bn]) for k in range(KC)], [uT, wb])
                        et = nxt("e", etmp)
                        if not smp:
                            P.op("pool", lambda e, j=j: e.tensor_copy(out=abuf[:, 0:2], in_=aH[:, j, :]), reads=[aH], writes=[abuf])
                            P.op("act", lambda e, pa=pa, bn=bn: e.copy(out=abuf[:, 2:2 + bn], in_=pa[:, 0:bn]), reads=[pa], writes=[abuf])
                            if halo:
                                P.op("dve", lambda e: e.tensor_scalar(out=abuf[:, 2:130], in0=abuf[:, 2:130], scalar1=small[:, S_PM:S_PM + 1],
                                                                      scalar2=None, op0=ALU.mult), reads=[abuf, small], writes=[abuf])
                            conv3(et[:, 0:bn], lambda o, bn=bn: abuf[:, o:o + bn], wfc, j, bn, et[:, 0:bn])
                            P.op("pool", lambda e, j=j, bn=bn: e.tensor_copy(out=aH[:, j, :], in_=abuf[:, bn:bn + 2]), reads=[abuf], writes=[aH])
                            P.op("act", lambda e, et=et, bn=bn: e.activation(out=et[:, 0:bn], in_=et[:, 0:bn], func=AF.Silu), reads=[et], writes=[et])
                            P.op("dve", lambda e, et=et, pg_=pg_, j=j, bn=bn: e.tensor_tensor(out=zT[:, j, 0:bn], in0=pg_[:, 0:bn], in1=et[:, 0:bn], op=ALU.mult),
                                 reads=[pg_, et], writes=[zT])
                        else:
                            P.op("pool", lambda e, j=j: e.tensor_copy(out=asx[:, :, 0:2], in_=sfcT[:, j, :, :]), reads=[sfcT], writes=[asx])
                            P.op("act", lambda e, pa=pa: e.copy(out=asx[:, :, 2:6], in_=pa[:, 0:TS].rearrange("p (b t) -> p b t", t=4)),
                                 reads=[pa], writes=[asx])
                            ev = et[:, 0:TS].rearrange("p (b t) -> p b t", t=4)
                            conv3(ev, lambda o: asx[:, :, o:o + 4], wfc, j, 4, ev)
                            P.op("pool", lambda e, j=j: e.tensor_copy(out=sfc_o[:, j, :, :], in_=asx[:, :, 4:6]), reads=[asx], writes=[sfc_o])
                            P.op("act", lambda e, et=et: e.activation(out=et[:, 0:TS], in_=et[:, 0:TS], func=AF.Silu), reads=[et], writes=[et])
                            P.op("dve", lambda e, et=et, pg_=pg_, j=j, b0=b0: e.tensor_tensor(out=zT[:, j, b0:b0 + TS], in0=pg_[:, 0:TS], in1=et[:, 0:TS], op=ALU.mult),
                                 reads=[pg_, et], writes=[zT])
            _ck(P, 'F3_' + str(p))
            ln_prefetch(1)
            proj_resid(w_down_r, JF, zT, tiles, G[2], G[3])
            for (ti, rows, col0, smp) in tiles:
                ln_affine(ti, rows, 1)
                if smp:
                    P.dma("act", y_s, R[0:rows, ti, :], reads=[Rt[ti]], is_out=True)
                else:
                    row0 = tok0 + ti * 128 - 1024
                    if row0 >= 0:
                        P.dma("act", y_p[row0:row0 + 128, :], R[0:rows, ti, :], reads=[Rt[ti]], is_out=True)
        _ck(P, 'final')
        P.dma("sp", o_pC, Cst[:, :, :, 0:256], reads=[Cst], is_out=True)
        P.op("act", lambda e: e.copy(out=sc[:, 0:8].rearrange("p (h c) -> p h c", c=2), in_=Cst[:, :, :, 256]), reads=[Cst], writes=[sc])
        P.op("dve", lambda e: e.tensor_tensor(out=sc[:, 8:12], in0=FM[:, 0:4], in1=FM[:, 4:8], op=ALU.add), reads=[FM], writes=[sc])
        P.dma("sp", o_pn, sc[:, 0:8], reads=[sc], is_out=True)
        P.dma("sp", o_pm, sc[:, 8:12], reads=[sc], is_out=True)
        P.dma("sp", o_pconv, prodH[:, :, :].rearrange("p a b -> p (a b)"), reads=[prodH], is_out=True)
        P.dma("sp", o_pfc, aH[:, :, :].rearrange("p a b -> p (a b)"), reads=[aH], is_out=True)
        P.dma("sp", o_sconv, sconv_o[:, :, :, :].rearrange("p a b c -> p (a b c)"), reads=[sconv_o], is_out=True)
        P.dma("sp", o_sfc, sfc_o[:, :, :, :].rearrange("p a b c -> p (a b c)"), reads=[sfc_o], is_out=True)
        P.emit()
        print("ops:", {e: len(v) for e, v in P.ops.items()})
    return nc


_NC = None


def kernel(x_prompt, x_sample, c_prompt, c_sample, state_conv, state_mlstm_C, state_mlstm_n, state_mlstm_m,
           state_ffn_conv, w_ada, b_ada, w_in, b_gate, w_conv, w_mh_norm, w_out, ln1_g, ln1_b, w_up, w_ffn_conv,
           w_down, ln2_g, ln2_b):
    global _NC
    f = lambda a: np.ascontiguousarray(np.asarray(a, dtype=np.float32))
    x_prompt, x_sample, c_prompt, c_sample = f(x_prompt), f(x_sample), f(c_prompt), f(c_sample)
    if _NC is None:
        _NC = build()
    nc = _NC
    con = make_consts()
    shared = {
        "con": con,
        "w_ada": f(w_ada[0]), "badaT": f(np.asarray(b_ada[0]).reshape(96, 128).T),
        "badarep": f(np.tile(np.concatenate([np.asarray(b_ada[0])[4096:6144], np.asarray(b_ada[0])[10240:12288]])[None, :], (128, 1))),
        "w_in": f(w_in[0]), "bgate": f(np.tile(np.asarray(b_gate[0])[None, :], (128, 1))),
        "wconvT": f(np.asarray(w_conv[0]).reshape(3, 8, 128).transpose(2, 1, 0).reshape(128, 24)),
        "wmhT": f(np.asarray(w_mh_norm[0]).reshape(8, 128).T),
        "w_out": f(w_out[0]), "w_up": f(w_up[0]),
        "wfcT": f(np.asarray(w_ffn_conv[0]).reshape(3, JF, 128).transpose(2, 1, 0).reshape(128, JF * 3)),
        "w_down": f(w_down[0]),
        "lnrep": f(np.stack([np.tile(np.asarray(v[0])[None, :], (128, 1)) for v in (ln1_g, ln1_b, ln2_g, ln2_b)])),
    }
    in_maps = []
    for c in range(8):
        bp = c // 2
        hh = c % 2
        sq = slice(16 * c, 16 * c + 16)
        if hh == 1:
            xpc = x_prompt[bp]
        else:
            xpc = np.concatenate([x_prompt[bp, 0:896], x_prompt[bp, 0:128], x_prompt[bp, 0:1024]], axis=0)
        call = np.concatenate([c_prompt[bp:bp + 1], c_sample[sq], np.repeat(c_prompt[bp:bp + 1], 128, 0),
                               np.repeat(c_sample[sq], 4, 0)], axis=0)
        cT = call.T.reshape(KC, 128, 209).transpose(1, 0, 2)
        m0 = np.asarray(state_mlstm_m[0])[sq]
        d = dict(shared)
        d.update({
            "xp": f(xpc), "pm": np.full((128, 1), float(hh), np.float32), "xs": f(x_sample[sq].reshape(TS, D)), "cT": f(cT),
            "m0tok": f(np.repeat(m0, 4, 0)), "m0rep": f(np.tile(m0.T.reshape(1, 64), (128, 1))),
            "C0": f(np.asarray(state_mlstm_C[0])[sq]),
            "n0T": f(np.asarray(state_mlstm_n[0])[sq].reshape(16, 4, 2, 128).transpose(3, 0, 1, 2).reshape(128, 128)),
            "sconvT": f(np.asarray(state_conv[0])[sq].reshape(16, 2, 8, 128).transpose(3, 2, 0, 1).reshape(128, 256)),
            "sfcT": f(np.asarray(state_ffn_conv[0])[sq].reshape(16, 2, JF, 128).transpose(3, 2, 0, 1).reshape(128, JF * 32)),
        })
        in_maps.append(d)
    res = run_bass_kernel_spmd(nc, in_maps, core_ids=list(range(8)))
    rs = res.results
    y_prompt = np.stack([np.concatenate([rs[2 * b]["y_p"], rs[2 * b + 1]["y_p"]], axis=0) for b in range(4)])
    y_sample = np.concatenate([rs[c]["y_s"].reshape(16, 4, D) for c in range(8)], 0)
    od = [1, 3, 5, 7]
    pconv = np.stack([rs[c]["o_pconv"].reshape(128, 8, 2).transpose(2, 1, 0).reshape(2, 1024) for c in od])[None]
    pC = np.stack([rs[c]["o_pC"].transpose(1, 2, 0, 3).reshape(4, 256, 256) for c in od])[None]
    pn = np.stack([rs[c]["o_pn"].reshape(128, 4, 2).transpose(1, 2, 0).reshape(4, 256) for c in od])[None]
    pm = np.stack([rs[c]["o_pm"][0] for c in od])[None]
    pfc = np.stack([rs[c]["o_pfc"].reshape(128, JF, 2).transpose(2, 1, 0).reshape(2, DFF) for c in od])[None]
    sconv = np.concatenate([rs[c]["o_sconv"].reshape(128, 8, 16, 2).transpose(2, 3, 1, 0).reshape(16, 2, 1024) for c in range(8)], 0)[None]
    sC = np.concatenate([rs[c]["o_sC"] for c in range(8)], 0)[None]
    sn = np.concatenate([rs[c]["o_sn"].reshape(128, 16, 4, 2).transpose(1, 2, 3, 0).reshape(16, 4, 256) for c in range(8)], 0)[None]
    sm = np.concatenate([rs[c]["o_sm"].reshape(16, 4, 4)[:, 3, :] for c in range(8)], 0)[None]
    sfc = np.concatenate([rs[c]["o_sfc"].reshape(128, JF, 16, 2).transpose(2, 3, 1, 0).reshape(16, 2, DFF) for c in range(8)], 0)[None]
    outs = (y_prompt, y_sample, pconv, pC, pn, pm, pfc, sconv, sC, sn, sm, sfc)
    return tuple(np.ascontiguousarray(o, dtype=np.float32) for o in outs)
```

```python
import numpy as np
from contextlib import ExitStack
import concourse.bass as bass
import concourse.mybir as mybir
from concourse.bass_utils import run_bass_kernel_spmd

F32 = mybir.dt.float32
BF16 = mybir.dt.bfloat16
ALU = mybir.AluOpType
AF = mybir.ActivationFunctionType
AX = mybir.AxisListType

D = 2048
KC = 16
DFF = 5632
JF = 44
NH = 4
HD = 256
NPASS = 4
TPP = 512
TS = 64
ALPHA = float(2 ** 0.25)
EPS = 1e-5
NEG = -1.0e30
INCOLS = 7176

C_ID, C_NEGC, C_NEGCT, C_TRI, C_ONE = 0, 128, 256, 384, 512
C_SNEGC, C_SNEGCT, C_SNEGB, C_STRI = 640, 704, 768, 832
C_BMREP, C_BM = 896, 1920
NCON = 1936


def make_consts():
    c = np.zeros((128, NCON), np.float32)
    i = np.arange(128)
    c[:, C_ID:C_ID + 128] = np.eye(128)
    c[:, C_NEGC:C_NEGC + 128] = np.where(i[None, :] <= i[:, None], 0.0, NEG)
    c[:, C_NEGCT:C_NEGCT + 128] = np.where(i[:, None] <= i[None, :], 0.0, NEG)
    c[:, C_TRI:C_TRI + 128] = (i[:, None] <= i[None, :]).astype(np.float32)
    c[:, C_ONE:C_ONE + 128] = 1.0
    j = np.arange(64)
    same = (j[:, None] // 4) == (j[None, :] // 4)
    c[:64, C_SNEGC:C_SNEGC + 64] = np.where(same & (j[None, :] <= j[:, None]), 0.0, NEG)
    c[:64, C_SNEGCT:C_SNEGCT + 64] = np.where(same & (j[:, None] <= j[None, :]), 0.0, NEG)
    c[:64, C_SNEGB:C_SNEGB + 64] = np.where(same, 0.0, NEG)
    c[:64, C_STRI:C_STRI + 64] = (same & (j[:, None] <= j[None, :])).astype(np.float32)
    bm = ((j[None, :] // 4) == np.arange(16)[:, None]).astype(np.float32)
    c[:, C_BMREP:C_BMREP + 1024] = np.tile(bm.reshape(1, 1024), (128, 1))
    c[:64, C_BM:C_BM + 16] = bm.T
    return c


class Buf:
    def __init__(self, name, t):
        self.name = name
        self.t = t
        self.w = []
        self.r = []
        self.dsem = None
        self.dcount = 0
        self.excl = False

    def __getitem__(self, k):
        return self.t[k]


class Prog:
    ENG = ["pe", "act", "dve", "pool", "sp"]

    def __init__(self, nc, stack):
        self.nc = nc
        self.stack = stack
        self.sem = {e: stack.enter_context(nc.semaphore("sem_" + e)) for e in self.ENG}
        self.cnt = {e: 0 for e in self.ENG}
        self.seen = {e: {} for e in self.ENG}
        self.ops = {e: [] for e in self.ENG}
        self.cur = 16512
        self.limit = 229344
        self.nalloc = 0
        self.out_toks = []
        self.dead = False

    def sbuf(self, name, shape, dt, at=None):
        esz = 4 if dt == F32 else 2
        n = 1
        for s in shape[1:]:
            n *= s
        size = n * esz
        if at is None:
            off = (self.cur + 31) // 32 * 32
            self.cur = off + size
            assert self.cur <= self.limit, ("SBUF overflow", name, self.cur)
        else:
            off = at
        self.nalloc += 1
        t = self.nc.alloc_sbuf_tensor_at("%s_%d" % (name, self.nalloc), list(shape), dt, offset=off)
        b = Buf(name, t)
        b.off = off
        b.size = size
        return b

    def psum(self, name, shape, dt=F32):
        t = self.stack.enter_context(self.nc.psum_tensor(name, list(shape), dt))
        b = Buf(name, t)
        b.excl = True
        return b

    def _wait(self, e, tok):
        sem, val = tok
        key = id(sem)
        if self.seen[e].get(key, 0) >= val:
            return
        self.seen[e][key] = val
        self.ops[e].append(("wait", sem, val))

    def _deps(self, e, reads, writes):
        for b in reads:
            for tok in b.w:
                self._wait(e, tok)
            if b.excl:
                for tok in b.r:
                    self._wait(e, tok)
        for b in writes:
            for tok in b.w:
                self._wait(e, tok)
            for tok in b.r:
                self._wait(e, tok)

    def _reg(self, tok, reads, writes):
        for b in reads:
            if len(b.r) > 24:
                d = {}
                for s, v in b.r:
                    if d.get(id(s), (None, 0))[1] < v:
                        d[id(s)] = (s, v)
                b.r = list(d.values())
            b.r.append(tok)
        for b in writes:
            b.w = [tok]
            b.r = []

    def op(self, e, fn, reads=(), writes=(), signal=True):
        if self.dead:
            return
        self._deps(e, reads, writes)
        if signal:
            self.cnt[e] += 1
            tok = (self.sem[e], self.cnt[e])
            self.ops[e].append(("op", fn, self.sem[e], 1))
            self._reg(tok, reads, writes)
        else:
            self.ops[e].append(("op", fn, None, 0))

    def dma(self, e, out_ap, in_ap, reads=(), writes=(), is_out=False, nodeps=False, owner=None):
        if self.dead:
            return
        if not nodeps:
            self._deps(e, reads, writes)
        if owner is None:
            owner = (list(writes) + list(reads))[0]
        if owner.dsem is None:
            owner.dsem = {}
            owner.dcount = {}
        if e not in owner.dsem:
            owner.dsem[e] = self.stack.enter_context(self.nc.semaphore("ds%d_%s_%s" % (self.nalloc, owner.name, e)))
            owner.dcount[e] = 0
            self.nalloc += 1
        owner.dcount[e] += 16
        dsem = owner.dsem[e]
        tok = (dsem, owner.dcount[e])
        self.ops[e].append(("op", lambda eng: eng.dma_start(out=out_ap, in_=in_ap), dsem, 16))
        self._reg(tok, reads, writes)
        if is_out:
            self.out_toks.append(tok)
        return tok

    def handoff(self, old, new):
        if self.dead:
            return
        d = {}
        for b in old:
            for s_, v in b.w + b.r:
                if d.get(id(s_), (None, 0))[1] < v:
                    d[id(s_)] = (s_, v)
        for b in new:
            dd = dict(d)
            for s_, v in b.r:
                if dd.get(id(s_), (None, 0))[1] < v:
                    dd[id(s_)] = (s_, v)
            b.r = list(dd.values())

    def emit(self):
        nc = self.nc
        ops = self.ops
        last = {}
        for s, v in self.out_toks:
            if last.get(id(s), (None, 0))[1] < v:
                last[id(s)] = (s, v)
        for tok in last.values():
            self._wait("sp", tok)

        def run(eng, lst):
            for o in lst:
                if o[0] == "wait":
                    eng.wait_ge(o[1], o[2])
                else:
                    ins = o[1](eng)
                    if o[2] is not None:
                        ins.then_inc(o[2], o[3])

        with nc.Block() as block:
            @block.tensor
            def _(eng):
                run(eng, ops["pe"])

            @block.scalar
            def _(eng):
                run(eng, ops["act"])

            @block.vector
            def _(eng):
                run(eng, ops["dve"])

            @block.gpsimd
            def _(eng):
                run(eng, ops["pool"])

            @block.sync
            def _(eng):
                run(eng, ops["sp"])


class _Stop(Exception):
    pass


STOP = None


def _ck(P, name):
    if STOP == name:
        P.dead = True


def build():
    nc = bass.Bass("TRN2", target_bir_lowering=False)

    def din(name, shape):
        return nc.dram_tensor(name, list(shape), F32, kind="ExternalInput").ap()

    def dout(name, shape):
        return nc.dram_tensor(name, list(shape), F32, kind="ExternalOutput").ap()

    xp = din("xp", [2048, D]); xs = din("xs", [TS, D])
    cT_d = din("cT", [128, KC, 209])
    con_d = din("con", [128, NCON])
    pm_d = din("pm", [128, 1])
    m0tok_d = din("m0tok", [TS, 4]); m0rep_d = din("m0rep", [128, 64])
    C0_d = din("C0", [16, 4, 256, 256]); n0T_d = din("n0T", [128, 128])
    sconvT_d = din("sconvT", [128, 8 * 32]); sfcT_d = din("sfcT", [128, JF * 32])
    w_ada = din("w_ada", [D, 6 * D]); badaT_d = din("badaT", [128, 96]); badarep_d = din("badarep", [128, 4096])
    w_in = din("w_in", [D, INCOLS]); bgate_d = din("bgate", [128, 8])
    wconvT_d = din("wconvT", [128, 24]); wmhT_d = din("wmhT", [128, 8])
    w_out = din("w_out", [D, D]); w_up = din("w_up", [D, 2 * DFF]); wfcT_d = din("wfcT", [128, JF * 3])
    w_down = din("w_down", [DFF, D])
    ln_d = din("lnrep", [4, 128, D])

    y_p = dout("y_p", [1024, D]); y_s = dout("y_s", [TS, D])
    o_pconv = dout("o_pconv", [128, 16]); o_pC = dout("o_pC", [128, 4, 2, 256]); o_pn = dout("o_pn", [128, 8])
    o_pm = dout("o_pm", [128, 4]); o_pfc = dout("o_pfc", [128, JF * 2])
    o_sconv = dout("o_sconv", [128, 8 * 32]); o_sC = dout("o_sC", [16, 4, 256, 256]); o_sn = dout("o_sn", [128, 128])
    o_sm = dout("o_sm", [TS, 4]); o_sfc = dout("o_sfc", [128, JF * 32])

    w_ada_r = w_ada.rearrange("(k p) c -> p k c", p=128)
    w_in_r = w_in.rearrange("(k p) c -> p k c", p=128)
    w_out_r = w_out.rearrange("(k p) c -> p k c", p=128)
    w_up_r = w_up.rearrange("(k p) c -> p k c", p=128)
    w_down_r = w_down.rearrange("(k p) c -> p k c", p=128)

    with ExitStack() as st:
        P = Prog(nc, st)
        CON = P.sbuf("con", [128, NCON], F32)
        identb = P.sbuf("identb", [128, 128], BF16)
        modT = P.sbuf("modT", [128, 4, KC, 17], F32)
        G = [P.sbuf("G%d" % i, [128, D], BF16) for i in range(4)]
        small = P.sbuf("small", [128, 512], F32)
        S_BADA, S_BG, S_WCV, S_WMH, S_WFC, S_EPS, S_M0T, S_M0R = 0, 96, 104, 128, 136, 268, 272, 276
        S_PM, S_PMOFF = 344, 345
        wg = P.sbuf("wg", [128, KC, 8], BF16)
        Cst = P.sbuf("Cst", [128, 4, 2, 257], F32)
        Cbf = P.sbuf("Cbf", [128, 4, 2, 257], BF16)
        FM = P.sbuf("FM", [128, 16], F32)
        prodH = P.sbuf("prodH", [128, 8, 2], F32)
        aH = P.sbuf("aH", [128, JF, 2], F32)
        sconvT = P.sbuf("sconvT", [128, 8, 16, 2], F32)
        sfcT = P.sbuf("sfcT", [128, JF, 16, 2], F32)
        n0T = P.sbuf("n0T", [128, 16, 4, 2], F32)
        sconv_o, sfc_o = sconvT, sfcT
        sn_o = Buf("sn_o", n0T.t)
        NTL = 4
        GT = P.sbuf("GT", [128, NTL, 8], F32)
        gs = {nm: P.sbuf("g_" + nm, [128, NTL, 4], F32) for nm in ["F", "g", "M", "inter", "wc", "dcr", "en", "mtok"]}
        dcrs = P.sbuf("dcrs", [128, 4, 16], F32)
        DT = P.sbuf("DT", [128, NTL, 4, 128], BF16)
        tmp4 = P.sbuf("tmp4", [128, 4, 128], F32)
        diag4 = P.sbuf("diag4", [128, 4, 128], F32)
        sc = P.sbuf("sc", [128, 64], F32)
        xn = P.sbuf("xn", [128, D], BF16)
        stt = P.sbuf("stt", [128, 4, 6], F32)
        mv = P.sbuf("mv", [128, 4], F32)
        etmp = [P.sbuf("etmp%d" % i, [128, 512], F32) for i in range(2)]
        brep = etmp
        qTs = P.sbuf("qTs", [128, 8, TS], BF16)
        sigs = P.sbuf("sigs", [128, 8, TS], BF16)
        As_sb = P.sbuf("As_sb", [TS, 4, 257], F32)
        NTOK = TPP
        R = P.sbuf("R", [128, NTL, D], F32)
        uT = P.sbuf("uT", [128, KC, NTOK], BF16)
        uTf = P.sbuf("uTf", [128, 2, D], F32, at=uT.off)
        Rt = [Buf("Rt%d" % i, None) for i in range(NTL)]
        WB = [P.sbuf("wb%d" % i, [128, KC, 512], BF16) for i in range(2)]
        zbase = P.cur
        yT = P.sbuf("yT", [128, KC, NTOK], BF16)
        sT = P.sbuf("sT", [128, KC, 209], BF16, at=yT.off)
        kw = P.sbuf("kw", [128, 1, NTL, 256], BF16)
        vext = P.sbuf("vext", [128, 1, NTL, 257], BF16)
        SwT = P.sbuf("SwT", [128, 128], BF16)
        Bs = P.sbuf("Bs", [128, 257], F32)
        numx = P.sbuf("numx", [128, 257], F32)
        hn = P.sbuf("hn", [128, 256], BF16)
        cbase = P.cur
        qT = P.sbuf("qT", [128, 2, NTOK], BF16)
        kT = P.sbuf("kT", [128, 2, NTOK], BF16)
        sigo = P.sbuf("sigo", [128, 2, NTOK], BF16)
        Csb = P.sbuf("Csb", [128, 512], F32)
        prod = P.sbuf("prod", [128, 520], F32)
        ctmp = P.sbuf("ctmp", [128, 512], F32)
        psx = P.sbuf("psx", [128, 16, 6], F32)
        cend = P.cur
        P.cur = cbase
        C0fH = [P.sbuf("C0f%d" % i, [128, 2, 2, 257], F32) for i in range(2)]
        C0bH = [P.sbuf("C0b%d" % i, [128, 2, 2, 257], BF16) for i in range(2)]
        QmR = [P.sbuf("Qm%d" % i, [128, 8, TS], BF16) for i in range(2)]
        kwmR = [P.sbuf("kwm%d" % i, [TS, 256], BF16) for i in range(2)]
        zend = max(P.cur, cend)
        mixer_bufs = [yT, kw, vext, qT, kT, sigo, SwT, Bs, numx, hn, Csb, prod, ctmp, psx] + C0fH + C0bH + QmR + kwmR
        conv_bufs = [Csb, prod, ctmp, psx]
        samp_bufs = C0fH + C0bH + QmR + kwmR
        head_bufs = [qT, kT, sigo]
        P.cur = zbase
        zT = P.sbuf("zT", [128, JF, NTOK], BF16)
        abuf = P.sbuf("abuf", [128, 520], F32)
        asx = P.sbuf("asx", [128, 16, 6], F32)
        ffn_bufs = [zT, abuf, asx]
        P.cur = max(P.cur, zend)
        print("SBUF used per partition:", P.cur, "of", P.limit)

        PA = [P.psum("pa%d" % i, [128, 512], F32) for i in range(6)]
        PT = [P.psum("pt%d" % i, [128, 8, 128], BF16) for i in range(2)]
        rot = {"a": 0, "t": 0, "w": 0, "l": 0, "b": 0, "e": 0}

        def nxt(key, lst):
            rot[key] = (rot[key] + 1) % len(lst)
            return lst[rot[key]]

        def con(c0, n, rows=128):
            return CON[0:rows, c0:c0 + n]

        def mm(ps, out_ap, pairs, reads):
            n = len(pairs)
            for i, (l, r) in enumerate(pairs):
                P.op("pe", lambda e, l=l, r=r, i=i: e.matmul(out_ap, lhsT=l, rhs=r, start=(i == 0), stop=(i == n - 1)),
                     reads=reads, writes=[ps], signal=(i == n - 1))

        def load_w(wb, pieces):
            first = True
            for c0, src in pieces:
                n = src.shape[-1]
                kk = src.shape[1]
                ksp = [(0, kk)] if kk * n <= 2048 else [(0, kk // 2), (kk // 2, kk)]
                for (ka, kb) in ksp:
                    P.dma("pool", wb[:, ka:kb, c0:c0 + n], src[:, ka:kb, :], writes=[wb], nodeps=not first)
                    first = False

        wcache = nc.dram_tensor("wcache", [62, 128, KC * 512], BF16).ap()
        cbufs = [Buf("cache%d" % i, None) for i in range(62)]
        ust = {"u": 0, "first": True}

        def get_unit(pieces, precached=None):
            wb = nxt("w", WB)
            u = ust["u"]
            ust["u"] += 1
            first_ = ust["first"]
            if precached is not None and u < 58:
                u = precached
                first_ = False
            flat = wb[:, :, :].rearrange("p k c -> p (k c)")
            if first_:
                load_w(wb, pieces)
                P.dma("sp", wcache[u], flat, reads=[wb], writes=[cbufs[u]], owner=wb)
            else:
                h = KC * 256
                P.dma("sp", flat[:, 0:h], wcache[u][:, 0:h], reads=[cbufs[u]], writes=[wb], owner=wb)
                P.dma("sp", flat[:, h:2 * h], wcache[u][:, h:2 * h], reads=[cbufs[u]], writes=[wb], owner=wb, nodeps=True)
            return wb

        BODY_START = True
        P.dma("sp", CON[:, :], con_d, writes=[CON])
        P.op("dve", lambda e: e.tensor_copy(out=identb[:, :], in_=CON[:, C_ID:C_ID + 128]), reads=[CON], writes=[identb])
        P.op("pool", lambda e: e.memset(small[:, :], 0.0), writes=[small])
        P.dma("sp", small[:, S_BADA:S_BADA + 96], badaT_d, writes=[small])
        P.dma("sp", small[:, S_BG:S_BG + 8], bgate_d, writes=[small])
        P.dma("sp", small[:, S_WCV:S_WCV + 24], wconvT_d, writes=[small])
        P.dma("sp", small[:, S_WMH:S_WMH + 8], wmhT_d, writes=[small])
        P.dma("sp", small[:, S_WFC:S_WFC + 132], wfcT_d, writes=[small])
        P.dma("sp", small[0:TS, S_M0T:S_M0T + 4], m0tok_d, writes=[small])
        P.dma("sp", small[:, S_M0R:S_M0R + 64], m0rep_d, writes=[small])
        P.op("dve", lambda e: e.memset(small[:, S_EPS:S_EPS + 1], EPS), reads=[], writes=[small])
        P.dma("sp", small[:, S_PM:S_PM + 1], pm_d, writes=[small])
        P.op("dve", lambda e: e.tensor_scalar(out=small[:, S_PMOFF:S_PMOFF + 1], in0=small[:, S_PM:S_PM + 1], scalar1=-1.0, scalar2=3.0e4,
                                              op0=ALU.add, op1=ALU.mult), reads=[small], writes=[small])
        P.dma("sp", sconvT[:, :, :, :].rearrange("p a b c -> p (a b c)"), sconvT_d, writes=[sconvT])
        P.dma("sp", sfcT[:, :, :, :].rearrange("p a b c -> p (a b c)"), sfcT_d, writes=[sfcT])
        P.dma("sp", n0T[:, :, :, :].rearrange("p a b c -> p (a b c)"), n0T_d, writes=[n0T])
        P.op("dve", lambda e: e.memset(Cst[:, :, :, :], 0.0), writes=[Cst])
        P.op("dve", lambda e: e.memset(Cbf[:, :, :, :], 0.0), writes=[Cbf])
        P.op("dve", lambda e: e.memset(FM[:, :], 0.0), writes=[FM])
        P.op("dve", lambda e: e.memset(prodH[:, :, :], 0.0), writes=[prodH])
        P.op("dve", lambda e: e.memset(aH[:, :, :], 0.0), writes=[aH])
        epsb = small[:, S_EPS:S_EPS + 1]
        Rflat = R[:, :, :].rearrange("p a b -> p (a b)")
        P.dma("sp", Rflat[:, 0:KC * 209], cT_d.rearrange("p k n -> p (k n)"), writes=Rt)
        P.op("act", lambda e: e.activation(out=sT[:, :, :].rearrange("p k n -> p (k n)"), in_=Rflat[:, 0:KC * 209], func=AF.Silu),
             reads=Rt, writes=[sT])
        load_w(wg, [(0, w_in_r[:, :, 7168:7176])])
        gi_of_group = {0: 0, 1: 1, 3: 2, 4: 3}
        def ada_units(ulist):
            for u in ulist:
                g, sub = u // 4, u % 4
                wb = nxt("w", WB)
                load_w(wb, [(0, w_ada_r[:, :, u * 512:(u + 1) * 512])])
                if g in gi_of_group:
                    gi = gi_of_group[g]
                    for cc in range(4):
                        j = sub * 4 + cc
                        ps = nxt("a", PA)
                        mm(ps, ps[:, 0:17], [(wb[:, k, cc * 128:(cc + 1) * 128], sT[:, k, 0:17]) for k in range(KC)], [wb, sT])
                        bcol = small[:, S_BADA + g * 16 + j:S_BADA + g * 16 + j + 1]
                        addc = 1.0 if g in (1, 4) else 0.0
                        P.op("dve", lambda e, ps=ps, gi=gi, j=j, bcol=bcol, addc=addc: e.tensor_scalar(
                            out=modT[:, gi, j, :], in0=ps[:, 0:17], scalar1=bcol, scalar2=addc, op0=ALU.add, op1=ALU.add),
                            reads=[ps, small], writes=[modT])
                else:
                    gq = 0 if g == 2 else 1
                    br = nxt("b", brep)
                    P.dma("sp", br[:, :], badarep_d[:, gq * 2048 + sub * 512:gq * 2048 + (sub + 1) * 512], writes=[br])
                    for smp in range(2):
                        rows = 128 if smp == 0 else TS
                        c0 = 17 if smp == 0 else 145
                        ps = nxt("a", PA)
                        mm(ps, ps[0:rows, :], [(sT[:, k, c0:c0 + rows], wb[:, k, :]) for k in range(KC)], [wb, sT])
                        Gb = G[gq * 2 + smp]
                        P.op("dve", lambda e, ps=ps, rows=rows, Gb=Gb, br=br, sub=sub: e.scalar_tensor_tensor(
                            out=Gb[0:rows, sub * 512:(sub + 1) * 512], in0=ps[0:rows, :], scalar=1.0, in1=br[0:rows, :],
                            op0=ALU.add, op1=ALU.add), reads=[ps, br], writes=[Gb])
                yield

        for _ in ada_units(range(0, 8)):
            pass
        ada_gen = ada_units(range(8, 24))

        def ln_stats(src_buf, ti, rows):
            for q in range(4):
                P.op("dve", lambda e, q=q: e.bn_stats(out=stt[0:rows, q, :], in_=src_buf[0:rows, ti, q * 512:(q + 1) * 512]),
                     reads=[Rt[ti]], writes=[stt])
            P.op("dve", lambda e: e.bn_aggr(out=mv[0:rows, 0:2], in_=stt[0:rows, :, :].rearrange("p a b -> p (a b)")),
                 reads=[stt], writes=[mv])
            P.op("act", lambda e: e.activation(out=mv[0:rows, 2:3], in_=mv[0:rows, 1:2], func=AF.Sqrt, bias=epsb[0:rows, :], scale=1.0),
                 reads=[mv, small], writes=[mv])
            P.op("dve", lambda e: e.reciprocal(out=mv[0:rows, 3:4], in_=mv[0:rows, 2:3]), reads=[mv], writes=[mv])

        def ln_transpose(ti, rows, col0, gsh, gsc, sample):
            ln_stats(R, ti, rows)
            P.op("dve", lambda e: e.tensor_scalar(out=xn[0:rows, :], in0=R[0:rows, ti, :], scalar1=mv[0:rows, 0:1],
                                                  scalar2=mv[0:rows, 3:4], op0=ALU.subtract, op1=ALU.mult),
                 reads=[Rt[ti], mv], writes=[xn])
            for half in range(2):
                pt = nxt("t", PT)
                for kk in range(8):
                    k = half * 8 + kk
                    P.op("pe", lambda e, k=k, kk=kk, pt=pt: e.transpose(out=pt[:, kk, 0:rows], in_=xn[0:rows, k * 128:(k + 1) * 128],
                                                                        identity=identb[0:rows, 0:rows]),
                         reads=[xn, identb], writes=[pt], signal=(kk == 7))
                for kk in range(8):
                    k = half * 8 + kk
                    if not sample:
                        if half == 0:
                            P.op("dve", lambda e, k=k, kk=kk, pt=pt: e.tensor_scalar(
                                out=uT[:, k, col0:col0 + rows], in0=pt[:, kk, 0:rows], scalar1=modT[:, gsc, k, 0:1],
                                scalar2=modT[:, gsh, k, 0:1], op0=ALU.mult, op1=ALU.add), reads=[pt, modT], writes=[uT])
                        else:
                            P.op("act", lambda e, k=k, kk=kk, pt=pt: e.activation(
                                out=uT[:, k, col0:col0 + rows], in_=pt[:, kk, 0:rows], func=AF.Identity,
                                bias=modT[:, gsh, k, 0:1], scale=modT[:, gsc, k, 0:1]), reads=[pt, modT], writes=[uT])
                    else:
                        et = nxt("e", etmp)
                        P.op("dve", lambda e, k=k, kk=kk, pt=pt, et=et: e.tensor_tensor(
                            out=et[:, 0:TS].rearrange("p (b t) -> p b t", t=4),
                            in0=pt[:, kk, 0:TS].rearrange("p (b t) -> p b t", t=4),
                            in1=modT[:, gsc, k, 1:17].unsqueeze(2).to_broadcast([128, 16, 4]), op=ALU.mult),
                            reads=[pt, modT], writes=[et])
                        P.op("dve", lambda e, k=k, et=et: e.tensor_tensor(
                            out=uT[:, k, col0:col0 + TS].rearrange("p (b t) -> p b t", t=4),
                            in0=et[:, 0:TS].rearrange("p (b t) -> p b t", t=4),
                            in1=modT[:, gsh, k, 1:17].unsqueeze(2).to_broadcast([128, 16, 4]), op=ALU.add),
                            reads=[et, modT], writes=[uT])

        def gate_math(ti, rows, sample, masked=False):
            Fb, gb, Mb, ib, wcb, dcb, enb, mtb = [gs[n] for n in ["F", "g", "M", "inter", "wc", "dcr", "en", "mtok"]]
            r = slice(0, rows)
            ea, lg = sc[r, 0:4], sc[r, 4:8]
            P.op("act", lambda e: e.activation(out=ea, in_=GT[r, ti, 4:8], func=AF.Exp, scale=-1.0), reads=[GT], writes=[sc])
            P.op("act", lambda e: e.activation(out=lg, in_=ea, func=AF.Ln, bias=1.0), reads=[sc], writes=[sc])
            if masked:
                P.op("dve", lambda e: e.tensor_scalar(out=lg, in0=lg, scalar1=small[r, S_PM:S_PM + 1], scalar2=None, op0=ALU.mult),
                     reads=[sc, small], writes=[sc])
            ps = nxt("a", PA)
            tri = con(C_STRI, TS, TS) if sample else con(C_TRI, 128)
            mm(ps, ps[r, 0:4], [(tri, lg)], [CON, sc])
            if sample:
                P.op("dve", lambda e: e.tensor_scalar(out=Fb[r, ti, :], in0=ps[r, 0:4], scalar1=-1.0, scalar2=None, op0=ALU.mult),
                     reads=[ps], writes=[Fb])
            else:
                ps2 = nxt("a", PA)
                mm(ps2, ps2[:, 0:4], [(con(C_ONE, 128), lg)], [CON, sc])
                P.op("dve", lambda e: e.scalar_tensor_tensor(out=Fb[:, ti, :], in0=ps[:, 0:4], scalar=-1.0, in1=FM[:, 0:4],
                                                             op0=ALU.mult, op1=ALU.add), reads=[ps, FM], writes=[Fb])
                P.op("dve", lambda e: e.scalar_tensor_tensor(out=FM[:, 0:4], in0=ps2[:, 0:4], scalar=-1.0, in1=FM[:, 0:4],
                                                             op0=ALU.mult, op1=ALU.add), reads=[ps2, FM], writes=[FM])
            P.op("dve", lambda e: e.tensor_tensor(out=gb[r, ti, :], in0=GT[r, ti, 0:4], in1=Fb[r, ti, :], op=ALU.subtract),
                 reads=[GT, Fb], writes=[gb])
            if masked:
                P.op("dve", lambda e: e.tensor_scalar(out=gb[r, ti, :], in0=gb[r, ti, :], scalar1=small[r, S_PM:S_PM + 1],
                                                      scalar2=small[r, S_PMOFF:S_PMOFF + 1], op0=ALU.mult, op1=ALU.add),
                     reads=[gb, small], writes=[gb])
            ident = con(C_ID, rows, rows)
            P.op("dve", lambda e: e.tensor_tensor(out=diag4[r, :, 0:rows], in0=ident.unsqueeze(1).to_broadcast([rows, 4, rows]),
                                                  in1=gb[r, ti, :].unsqueeze(2).to_broadcast([rows, 4, rows]), op=ALU.mult),
                 reads=[CON, gb], writes=[diag4])
            pg = nxt("a", PA)
            pgv = pg[:, :].rearrange("p (h s) -> p h s", h=4)
            for h in range(4):
                P.op("pe", lambda e, h=h: e.matmul(pgv[:, h, 0:rows], lhsT=CON[r, C_ONE:C_ONE + 128], rhs=diag4[r, h, 0:rows],
                                                   start=True, stop=True), reads=[CON, diag4], writes=[pg], signal=(h == 3))
            negc = con(C_SNEGC, TS, TS) if sample else con(C_NEGC, 128)
            P.op("dve", lambda e: e.tensor_tensor(out=tmp4[r, :, 0:rows], in0=pgv[r, :, 0:rows],
                                                  in1=negc.unsqueeze(1).to_broadcast([rows, 4, rows]), op=ALU.add),
                 reads=[pg, CON], writes=[tmp4])
            mloc = sc[r, 8:12]
            P.op("dve", lambda e: e.tensor_reduce(out=mloc, in_=tmp4[r, :, 0:rows], axis=AX.X, op=ALU.max), reads=[tmp4], writes=[sc])
            if sample:
                m0t = small[r, S_M0T:S_M0T + 4]
                mcur = m0t
                P.op("dve", lambda e: e.tensor_tensor(out=tmp4[r, :, 0:rows], in0=pgv[r, :, 0:rows],
                                                      in1=con(C_SNEGB, TS, TS).unsqueeze(1).to_broadcast([rows, 4, rows]), op=ALU.add),
                     reads=[pg, CON, sc], writes=[tmp4])
                mend = sc[r, 12:16]
                P.op("dve", lambda e: e.tensor_reduce(out=mend, in_=tmp4[r, :, 0:rows], axis=AX.X, op=ALU.max), reads=[tmp4], writes=[sc])
                P.op("dve", lambda e: e.tensor_tensor(out=mend, in0=mend, in1=m0t, op=ALU.max), reads=[sc, small], writes=[sc])
                mrep = sc[:, 32:48]
                for h in range(4):
                    P.op("dve", lambda e, h=h: e.tensor_reduce(out=sc[:, 48:64], in_=pgv[:, h, 0:TS].rearrange("p (b t) -> p b t", t=4),
                                                               axis=AX.X, op=ALU.max), reads=[pg], writes=[sc])
                    m0r = small[:, S_M0R + h * 16:S_M0R + (h + 1) * 16]
                    P.op("dve", lambda e, m0r=m0r: e.tensor_tensor(out=sc[:, 48:64], in0=sc[:, 48:64], in1=m0r, op=ALU.max),
                         reads=[sc, small], writes=[sc])
                    P.op("dve", lambda e, m0r=m0r: e.tensor_tensor(out=sc[:, 48:64], in0=m0r, in1=sc[:, 48:64], op=ALU.subtract),
                         reads=[sc, small], writes=[sc])
                    P.op("act", lambda e, h=h: e.activation(out=dcrs[:, h, :], in_=sc[:, 48:64], func=AF.Exp), reads=[sc], writes=[dcrs])
            else:
                mcur = FM[:, 4:8]
                tmax = sc[:, 12:16]
                P.op("dve", lambda e: e.tensor_reduce(out=tmax, in_=pgv[:, :, :], axis=AX.X, op=ALU.max), reads=[pg], writes=[sc])
                mend = FM[:, 12:16]
                P.op("dve", lambda e: e.tensor_tensor(out=mend, in0=mcur, in1=tmax, op=ALU.max), reads=[FM, sc], writes=[FM])
            P.op("dve", lambda e: e.tensor_tensor(out=Mb[r, ti, :], in0=mloc, in1=mcur, op=ALU.max), reads=[sc, FM, small], writes=[Mb])
            d1, d2, d3, d4 = sc[r, 16:20], sc[r, 20:24], sc[r, 24:28], sc[r, 28:32]
            P.op("dve", lambda e: e.tensor_tensor(out=d1, in0=mcur, in1=Mb[r, ti, :], op=ALU.subtract), reads=[FM, small, Mb], writes=[sc])
            P.op("act", lambda e: e.activation(out=ib[r, ti, :], in_=d1, func=AF.Exp), reads=[sc], writes=[ib])
            P.op("dve", lambda e: e.tensor_tensor(out=d2, in0=gb[r, ti, :], in1=mend, op=ALU.subtract), reads=[gb, FM, sc], writes=[sc])
            P.op("act", lambda e: e.activation(out=wcb[r, ti, :], in_=d2, func=AF.Exp), reads=[sc], writes=[wcb])
            if not sample:
                P.op("dve", lambda e: e.tensor_tensor(out=d3, in0=mcur, in1=mend, op=ALU.subtract), reads=[FM], writes=[sc])
                P.op("act", lambda e: e.activation(out=dcb[:, ti, :], in_=d3, func=AF.Exp), reads=[sc], writes=[dcb])
            P.op("dve", lambda e: e.tensor_tensor(out=mtb[r, ti, :], in0=Fb[r, ti, :], in1=Mb[r, ti, :], op=ALU.add), reads=[Fb, Mb], writes=[mtb])
            P.op("act", lambda e: e.activation(out=enb[r, ti, :], in_=mtb[r, ti, :], func=AF.Exp, scale=-1.0), reads=[mtb], writes=[enb])
            P.op("dve", lambda e: e.tensor_tensor(out=diag4[r, :, 0:rows], in0=ident.unsqueeze(1).to_broadcast([rows, 4, rows]),
                                                  in1=Mb[r, ti, :].unsqueeze(2).to_broadcast([rows, 4, rows]), op=ALU.mult),
                 reads=[CON, Mb, pg], writes=[diag4])
            pm_ = nxt("a", PA)
            pmv = pm_[:, :].rearrange("p (h s) -> p h s", h=4)
            for h in range(4):
                P.op("pe", lambda e, h=h: e.matmul(pmv[r, h, 0:rows], lhsT=CON[r, C_ONE:C_ONE + rows], rhs=diag4[r, h, 0:rows],
                                                   start=True, stop=True), reads=[CON, diag4], writes=[pm_], signal=(h == 3))
            negct = con(C_SNEGCT, TS, TS) if sample else con(C_NEGCT, 128)
            P.op("dve", lambda e: e.scalar_tensor_tensor(out=tmp4[r, :, 0:rows], in0=pmv[r, :, 0:rows], scalar=-1.0,
                                                         in1=negct.unsqueeze(1).to_broadcast([rows, 4, rows]), op0=ALU.mult, op1=ALU.add),
                 reads=[pm_, CON], writes=[tmp4])
            for h in range(4):
                P.op("act", lambda e, h=h: e.activation(out=DT[r, ti, h, 0:rows], in_=tmp4[r, h, 0:rows], func=AF.Exp,
                                                        bias=gb[r, ti, h:h + 1], scale=1.0), reads=[tmp4, gb], writes=[DT])
            if not sample:
                P.op("dve", lambda e: e.tensor_copy(out=FM[:, 4:8], in_=FM[:, 12:16]), reads=[FM, sc], writes=[FM])

        def finalize(h, ti, rows, col0, en_ap):
            r = slice(0, rows)
            a = sc[r, 0:1]
            P.op("dve", lambda e: e.scalar_tensor_tensor(out=a, in0=numx[r, 256:257], scalar=-1.0, in1=numx[r, 256:257],
                                                         op0=ALU.mult, op1=ALU.max), reads=[numx], writes=[sc])
            P.op("dve", lambda e: e.tensor_tensor(out=a, in0=a, in1=en_ap, op=ALU.max), reads=[sc, gs["en"]], writes=[sc])
            P.op("dve", lambda e: e.reciprocal(out=sc[r, 1:2], in_=a), reads=[sc], writes=[sc])
            P.op("dve", lambda e: e.bn_stats(out=stt[r, 0, :], in_=numx[r, 0:256]), reads=[numx], writes=[stt])
            P.op("dve", lambda e: e.bn_aggr(out=mv[r, 0:2], in_=stt[r, 0, :]), reads=[stt], writes=[mv])
            P.op("dve", lambda e: e.tensor_tensor(out=sc[r, 2:3], in0=sc[r, 1:2], in1=sc[r, 1:2], op=ALU.mult), reads=[sc], writes=[sc])
            P.op("dve", lambda e: e.tensor_tensor(out=sc[r, 2:3], in0=sc[r, 2:3], in1=mv[r, 1:2], op=ALU.mult), reads=[sc, mv], writes=[sc])
            P.op("act", lambda e: e.activation(out=sc[r, 3:4], in_=sc[r, 2:3], func=AF.Sqrt, bias=epsb[r, :], scale=1.0),
                 reads=[sc, small], writes=[sc])
            P.op("dve", lambda e: e.reciprocal(out=sc[r, 4:5], in_=sc[r, 3:4]), reads=[sc], writes=[sc])
            P.op("dve", lambda e: e.tensor_tensor(out=sc[r, 4:5], in0=sc[r, 4:5], in1=sc[r, 1:2], op=ALU.mult), reads=[sc], writes=[sc])
            P.op("dve", lambda e: e.tensor_scalar(out=hn[r, :], in0=numx[r, 0:256], scalar1=mv[r, 0:1], scalar2=sc[r, 4:5],
                                                  op0=ALU.subtract, op1=ALU.mult), reads=[numx, mv, sc], writes=[hn])
            pt = nxt("t", PT)
            for c in range(2):
                P.op("pe", lambda e, c=c: e.transpose(out=pt[:, c, 0:rows], in_=hn[r, c * 128:(c + 1) * 128], identity=identb[r, r]),
                     reads=[hn, identb], writes=[pt], signal=(c == 1))
            for c in range(2):
                sg = sigs[:, 2 * h + c, :] if rows == TS else sigo[:, c, col0:col0 + rows]
                P.op("dve", lambda e, c=c, sg=sg: e.scalar_tensor_tensor(
                    out=yT[:, 8 + 2 * h + c, col0:col0 + rows], in0=pt[:, c, 0:rows],
                    scalar=small[:, S_WMH + 2 * h + c:S_WMH + 2 * h + c + 1], in1=sg, op0=ALU.mult, op1=ALU.mult),
                    reads=[pt, small, sigo, sigs], writes=[yT])

        def proj_resid(w_r, nk, lhs_buf, tiles, Gp, Gs):
            for nb in range(4):
                groups = [(k0, min(k0 + KC, nk)) for k0 in range(0, nk, KC)]
                acc = {}
                for gi_, (k0, k1) in enumerate(groups):
                    wb = get_unit([(0, w_r[:, k0:k1, nb * 512:(nb + 1) * 512])])
                    for (ti, rows, col0, smp) in tiles:
                        if gi_ == 0:
                            acc[ti] = PA[ti] if len(groups) > 1 else nxt("a", PA)
                        ps = acc[ti]
                        for k in range(k0, k1):
                            P.op("pe", lambda e, ps=ps, k=k, k0=k0, wb=wb, rows=rows, col0=col0: e.matmul(
                                ps[0:rows, :], lhsT=lhs_buf[:, k, col0:col0 + rows], rhs=wb[:, k - k0, :],
                                start=(k == 0), stop=(k == nk - 1)), reads=[lhs_buf, wb], writes=[ps], signal=(k == k1 - 1))
                for (ti, rows, col0, smp) in tiles:
                    ps = acc[ti]
                    Gb = Gs if smp else Gp
                    et = nxt("e", etmp)
                    P.op("dve", lambda e, ps=ps, rows=rows, Gb=Gb, et=et, nb=nb: e.tensor_tensor(
                        out=et[0:rows, :], in0=ps[0:rows, :], in1=Gb[0:rows, nb * 512:(nb + 1) * 512], op=ALU.mult),
                        reads=[ps, Gb], writes=[et])
                    P.op("dve", lambda e, ti=ti, rows=rows, et=et, nb=nb: e.scalar_tensor_tensor(
                        out=R[0:rows, ti, nb * 512:(nb + 1) * 512], in0=R[0:rows, ti, nb * 512:(nb + 1) * 512], scalar=ALPHA,
                        in1=et[0:rows, :], op0=ALU.mult, op1=ALU.add), reads=[Rt[ti], et], writes=[Rt[ti]])

        def ln_affine(ti, rows, li):
            ln_stats(R, ti, rows)
            P.op("dve", lambda e: e.tensor_scalar(out=R[0:rows, ti, :], in0=R[0:rows, ti, :], scalar1=mv[0:rows, 0:1],
                                                  scalar2=mv[0:rows, 3:4], op0=ALU.subtract, op1=ALU.mult),
                 reads=[Rt[ti], mv], writes=[Rt[ti]])
            P.op("dve", lambda e: e.tensor_tensor(out=R[0:rows, ti, :], in0=R[0:rows, ti, :], in1=uTf[0:rows, 0, :], op=ALU.mult),
                 reads=[Rt[ti], uT], writes=[Rt[ti]])
            P.op("pool", lambda e: e.tensor_tensor(out=R[0:rows, ti, :], in0=R[0:rows, ti, :], in1=uTf[0:rows, 1, :], op=ALU.add),
                 reads=[Rt[ti], uT], writes=[Rt[ti]])

        def ln_prefetch(li):
            P.dma("sp", uTf[:, 0, :], ln_d[2 * li], writes=[uT])
            P.dma("sp", uTf[:, 1, :], ln_d[2 * li + 1], writes=[uT], nodeps=True)

        wcv = lambda j, i: small[:, S_WCV + j * 3 + i:S_WCV + j * 3 + i + 1]
        wfc = lambda j, i: small[:, S_WFC + j * 3 + i:S_WFC + j * 3 + i + 1]

        def conv3(out_ap, src, wfn, j, n, tmp_ap):
            P.op("dve", lambda e: e.tensor_scalar(out=tmp_ap, in0=src(0), scalar1=wfn(j, 0), scalar2=None, op0=ALU.mult),
                 reads=[prod, abuf, psx, asx, small], writes=[ctmp, etmp[0], etmp[1]])
            P.op("dve", lambda e: e.scalar_tensor_tensor(out=tmp_ap, in0=src(1), scalar=wfn(j, 1), in1=tmp_ap, op0=ALU.mult, op1=ALU.add),
                 reads=[prod, abuf, psx, asx, small], writes=[ctmp, etmp[0], etmp[1]])
            P.op("dve", lambda e: e.scalar_tensor_tensor(out=out_ap, in0=src(2), scalar=wfn(j, 2), in1=tmp_ap, op0=ALU.mult, op1=ALU.add),
                 reads=[prod, abuf, psx, asx, small], writes=[ctmp, etmp[0], etmp[1]])

        _ck(P, 'ada')
        kws = P.sbuf("kws", [TS, 4, 256], BF16)
        vexts = P.sbuf("vexts", [TS, 4, 257], BF16)
        print("SBUF used per partition (final):", P.cur, "of", P.limit)
        first = True
        first_pre = True
        first_main = True
        passes = [("pre", 0, 4), ("pre", 512, 3), ("main", 896, 4), ("mains", 1408, 3), ("main", 1792, 2)]
        for pi, (kind, tok0, nt) in enumerate(passes):
            p = "%s%d" % (kind, pi)
            has_s = (kind == "mains")
            pre = (kind == "pre")
            halo = (kind == "main" and tok0 == 896)
            tiles = [(ti, 128, ti * 128, False) for ti in range(nt)]
            blocks = [(0, nt * 128, False)]
            s_ti, s_col = nt, nt * 128
            if has_s:
                tiles = tiles + [(s_ti, TS, s_col, True)]
                blocks = blocks + [(s_col, TS, True)]
            if not first:
                P.handoff(ffn_bufs, mixer_bufs)
            first = False
            if not pre and first_main:
                for _ in ada_gen:
                    pass
                P.handoff([sT], [yT])
            if pre:
                ust["u"] = 58
                ust["first"] = first_pre
                first_pre = False
            else:
                ust["u"] = 0
                ust["first"] = first_main
                first_main = False
            _ck(P, 'A_' + str(p))
            for (ti, rows, col0, smp) in tiles:
                src = xs if smp else xp[tok0 + ti * 128:tok0 + (ti + 1) * 128, :]
                P.dma("sp", R[0:rows, ti, :], src, writes=[Rt[ti]])
                ln_transpose(ti, rows, col0, 0, 1, smp)
            _ck(P, 'B_' + str(p))
            P.op("pool", lambda e: e.memset(vext[:, :, :, 256:257], 1.0), writes=[vext])
            for (ti, rows, col0, smp) in tiles:
                ps = nxt("a", PA)
                mm(ps, ps[0:rows, 0:8], [(uT[:, k, col0:col0 + rows], wg[:, k, :]) for k in range(KC)], [uT, wg])
                P.op("dve", lambda e, ps=ps, ti=ti, rows=rows: e.tensor_tensor(out=GT[0:rows, ti, :], in0=ps[0:rows, 0:8],
                                                                              in1=small[0:rows, S_BG:S_BG + 8], op=ALU.add),
                     reads=[ps, small], writes=[GT])
                gate_math(ti, rows, smp, masked=(pre or (halo and ti == 0)))
            _ck(P, 'C_' + str(p))
            for h in range(NH):
                wb = get_unit([(0, w_in_r[:, :, 4096 + h * 256:4096 + (h + 1) * 256]), (256, w_in_r[:, :, 5120 + h * 256:5120 + (h + 1) * 256])],
                              precached=58 + h)
                for (ti, rows, col0, smp) in tiles:
                    ps = nxt("a", PA)
                    mm(ps, ps[0:rows, :], [(uT[:, k, col0:col0 + rows], wb[:, k, :]) for k in range(KC)], [uT, wb])
                    P.op("dve", lambda e, ps=ps, ti=ti, rows=rows, h=h: e.tensor_scalar(
                        out=kw[0:rows, 0, ti, :], in0=ps[0:rows, 0:256], scalar1=gs["wc"][0:rows, ti, h:h + 1], scalar2=1.0 / 16.0,
                        op0=ALU.mult, op1=ALU.mult), reads=[ps, gs["wc"]], writes=[kw])
                    P.op("act", lambda e, ps=ps, ti=ti, rows=rows: e.copy(out=vext[0:rows, 0, ti, 0:256], in_=ps[0:rows, 256:512]),
                         reads=[ps], writes=[vext])
                if pre:
                    for (ti, rows, col0, smp) in tiles:
                        for c in range(2):
                            ps_u = nxt("a", PA)
                            mm(ps_u, ps_u[:, 0:257], [(kw[:, 0, ti, c * 128:(c + 1) * 128], vext[:, 0, ti, :])], [kw, vext])
                            P.op("dve", lambda e, ps_u=ps_u, c=c, ti=ti, h=h: e.scalar_tensor_tensor(
                                out=Cst[:, h, c, :], in0=Cst[:, h, c, :], scalar=gs["dcr"][:, ti, h:h + 1], in1=ps_u[:, 0:257],
                                op0=ALU.mult, op1=ALU.add), reads=[Cst, gs["dcr"], ps_u], writes=[Cst])
                        next(ada_gen, None)
                    P.op("act", lambda e, h=h: e.copy(out=Cbf[:, h, :, :], in_=Cst[:, h, :, :]), reads=[Cst], writes=[Cbf])
                    continue
                _ck(P, 'Ckv_%s_%d' % (p, h))
                wb = get_unit([(0, w_in_r[:, :, 3072 + h * 256:3072 + (h + 1) * 256]), (256, w_in_r[:, :, 4096 + h * 256:4096 + (h + 1) * 256])])
                wb2 = get_unit([(0, w_in_r[:, :, 6144 + h * 256:6144 + (h + 1) * 256])])
                for cc in range(6):
                    wsel = wb if cc < 4 else wb2
                    wc0 = (cc % 4) * 128
                    for (b0, bn, smp) in blocks:
                        ps = nxt("a", PA)
                        mm(ps, ps[:, 0:bn], [(wsel[:, k, wc0:wc0 + 128], uT[:, k, b0:b0 + bn]) for k in range(KC)], [uT, wsel])
                        if cc < 2:
                            P.op("act", lambda e, ps=ps, cc=cc, b0=b0, bn=bn: e.copy(out=qT[:, cc, b0:b0 + bn], in_=ps[:, 0:bn]),
                                 reads=[ps], writes=[qT])
                        elif cc < 4:
                            P.op("act", lambda e, ps=ps, cc=cc, b0=b0, bn=bn: e.mul(out=kT[:, cc - 2, b0:b0 + bn], in_=ps[:, 0:bn], mul=1.0 / 16.0),
                                 reads=[ps], writes=[kT])
                        else:
                            P.op("act", lambda e, ps=ps, cc=cc, b0=b0, bn=bn: e.activation(out=sigo[:, cc - 4, b0:b0 + bn], in_=ps[:, 0:bn], func=AF.Sigmoid),
                                 reads=[ps], writes=[sigo])
                _ck(P, 'Cfm_%s_%d' % (p, h))
                for (ti, rows, col0, smp) in tiles:
                    r = slice(0, rows)
                    ps_s = nxt("a", PA)
                    mm(ps_s, ps_s[r, 0:rows], [(kT[:, c, col0:col0 + rows], qT[:, c, col0:col0 + rows]) for c in range(2)], [kT, qT])
                    P.op("dve", lambda e, ps_s=ps_s, r=r, rows=rows, ti=ti, h=h: e.tensor_tensor(
                        out=SwT[r, 0:rows], in0=ps_s[r, 0:rows], in1=DT[r, ti, h, 0:rows], op=ALU.mult), reads=[ps_s, DT], writes=[SwT])
                    ps_a = nxt("a", PA)
                    mm(ps_a, ps_a[r, 0:257], [(SwT[r, 0:rows], vext[r, 0, ti, :])], [SwT, vext])
                    if smp:
                        P.op("act", lambda e, ps_a=ps_a, h=h: e.copy(out=As_sb[:, h, :], in_=ps_a[0:TS, 0:257]), reads=[ps_a], writes=[As_sb])
                        P.op("act", lambda e, h=h, col0=col0: e.copy(out=qTs[:, 2 * h:2 * h + 2, :], in_=qT[:, :, col0:col0 + TS]), reads=[qT], writes=[qTs])
                        P.op("act", lambda e, h=h, col0=col0: e.copy(out=sigs[:, 2 * h:2 * h + 2, :], in_=sigo[:, :, col0:col0 + TS]), reads=[sigo], writes=[sigs])
                        P.op("act", lambda e, h=h, ti=ti: e.copy(out=kws[:, h, :], in_=kw[0:TS, 0, ti, :]), reads=[kw], writes=[kws])
                        P.op("act", lambda e, h=h, ti=ti: e.copy(out=vexts[:, h, :], in_=vext[0:TS, 0, ti, :]), reads=[vext], writes=[vexts])
                        continue
                    ps_b = nxt("a", PA)
                    mm(ps_b, ps_b[:, 0:257], [(qT[:, c, col0:col0 + 128], Cbf[:, h, c, :]) for c in range(2)], [qT, Cbf])
                    P.op("act", lambda e, ps_b=ps_b, ti=ti, h=h: e.activation(out=Bs[:, :], in_=ps_b[:, 0:257], func=AF.Copy,
                                                                              scale=gs["inter"][:, ti, h:h + 1]), reads=[ps_b, gs["inter"]], writes=[Bs])
                    P.op("dve", lambda e, ps_a=ps_a: e.tensor_tensor(out=numx[:, :], in0=ps_a[:, 0:257], in1=Bs[:, :], op=ALU.add),
                         reads=[ps_a, Bs], writes=[numx])
                    finalize(h, ti, 128, col0, gs["en"][:, ti, h:h + 1])
                    for c in range(2):
                        ps_u = nxt("a", PA)
                        mm(ps_u, ps_u[:, 0:257], [(kw[:, 0, ti, c * 128:(c + 1) * 128], vext[:, 0, ti, :])], [kw, vext])
                        P.op("dve", lambda e, ps_u=ps_u, c=c, ti=ti, h=h: e.scalar_tensor_tensor(
                            out=Cst[:, h, c, :], in0=Cst[:, h, c, :], scalar=gs["dcr"][:, ti, h:h + 1], in1=ps_u[:, 0:257],
                            op0=ALU.mult, op1=ALU.add), reads=[Cst, gs["dcr"], ps_u], writes=[Cst])
                    P.op("act", lambda e, h=h: e.copy(out=Cbf[:, h, :, :], in_=Cst[:, h, :, :]), reads=[Cst], writes=[Cbf])
            if pre:
                continue
            _ck(P, 'C2_' + str(p))
            if has_s:
                P.handoff(head_bufs + conv_bufs, samp_bufs)
                accb = [PA[0], PA[1], PA[2], PA[3]]
                for b in range(16):
                    Qm = QmR[b % 2]
                    P.op("dve", lambda e, b=b, Qm=Qm: e.tensor_tensor(out=Qm[:, :, :], in0=qTs[:, :, :],
                                                                      in1=CON[:, C_BMREP + b * 64:C_BMREP + (b + 1) * 64].unsqueeze(1).to_broadcast([128, 8, TS]),
                                                                      op=ALU.mult), reads=[qTs, CON], writes=[Qm])
                    for hf in range(2):
                        Cf, Cb = C0fH[hf], C0bH[hf]
                        P.dma("sp", Cf[:, :, :, 0:256], C0_d[b, 2 * hf:2 * hf + 2].rearrange("h (c q) e -> q h c e", q=128), writes=[Cf])
                        P.op("dve", lambda e, b=b, hf=hf, Cf=Cf: e.tensor_copy(out=Cf[:, :, :, 256:257], in_=n0T[:, b, 2 * hf:2 * hf + 2, :].unsqueeze(3)),
                             reads=[n0T], writes=[Cf])
                        P.op("act", lambda e, Cf=Cf, Cb=Cb: e.copy(out=Cb[:, :, :, :], in_=Cf[:, :, :, :]), reads=[Cf], writes=[Cb])
                        for hl in range(2):
                            h = 2 * hf + hl
                            ps = accb[h]
                            for c in range(2):
                                P.op("pe", lambda e, ps=ps, h=h, hl=hl, c=c, b=b, Qm=Qm, Cb=Cb: e.matmul(
                                    ps[0:TS, 0:257], lhsT=Qm[:, 2 * h + c, :], rhs=Cb[:, hl, c, :],
                                    start=(b == 0 and c == 0), stop=(b == 15 and c == 1)),
                                    reads=[Qm, Cb], writes=[ps], signal=(c == 1))
                        for hl in range(2):
                            h = 2 * hf + hl
                            kwm = kwmR[hl]
                            P.op("dve", lambda e, h=h, b=b, kwm=kwm: e.tensor_scalar(out=kwm[:, :], in0=kws[:, h, :], scalar1=CON[0:TS, C_BM + b:C_BM + b + 1],
                                                                                    scalar2=None, op0=ALU.mult), reads=[kws, CON], writes=[kwm])
                            for c in range(2):
                                ps_u = PA[4 + c]
                                mm(ps_u, ps_u[:, 0:257], [(kwm[:, c * 128:(c + 1) * 128], vexts[:, h, :])], [kwm, vexts])
                                P.op("dve", lambda e, ps_u=ps_u, c=c, h=h, hl=hl, b=b, Cf=Cf: e.scalar_tensor_tensor(
                                    out=Cf[:, hl, c, :], in0=Cf[:, hl, c, :], scalar=dcrs[:, h, b:b + 1], in1=ps_u[:, 0:257],
                                    op0=ALU.mult, op1=ALU.add), reads=[Cf, dcrs, ps_u, Cb], writes=[Cf])
                        P.op("act", lambda e, b=b, hf=hf, Cf=Cf: e.copy(out=sn_o[:, b, 2 * hf:2 * hf + 2, :], in_=Cf[:, :, :, 256]), reads=[Cf], writes=[sn_o])
                        P.dma("act", o_sC[b, 2 * hf:2 * hf + 2].rearrange("h (c q) e -> q h c e", q=128), Cf[:, :, :, 0:256], reads=[Cf], is_out=True)
                for h in range(NH):
                    P.op("act", lambda e, h=h, s_ti=s_ti, accb=accb: e.activation(out=Bs[0:TS, :], in_=accb[h][0:TS, 0:257], func=AF.Copy,
                                                            scale=gs["inter"][0:TS, s_ti, h:h + 1]), reads=[accb[h], gs["inter"]], writes=[Bs])
                    P.op("dve", lambda e, h=h: e.tensor_tensor(out=numx[0:TS, :], in0=As_sb[:, h, :], in1=Bs[0:TS, :], op=ALU.add),
                         reads=[As_sb, Bs], writes=[numx])
                    finalize(h, s_ti, TS, s_col, gs["en"][0:TS, s_ti, h:h + 1])
                P.dma("sp", o_sn, sn_o[:, :, :, :].rearrange("p a b c -> p (a b c)"), reads=[sn_o], is_out=True)
                P.dma("sp", o_sm, gs["mtok"][0:TS, s_ti, :], reads=[gs["mtok"]], is_out=True)
                P.handoff(samp_bufs, head_bufs + conv_bufs)
            _ck(P, 'D_' + str(p))
            for j in range(8):
                wb = get_unit([(0, w_in_r[:, :, j * 128:(j + 1) * 128]), (128, w_in_r[:, :, 1024 + j * 128:1024 + (j + 1) * 128]),
                               (256, w_in_r[:, :, 2048 + j * 128:2048 + (j + 1) * 128])])
                for (b0, bn, smp) in blocks:
                    pss = []
                    for q in range(3):
                        ps = nxt("a", PA)
                        mm(ps, ps[:, 0:bn], [(wb[:, k, q * 128:(q + 1) * 128], uT[:, k, b0:b0 + bn]) for k in range(KC)], [uT, wb])
                        pss.append(ps)
                    pB, pC, pH = pss
                    P.op("act", lambda e, pC=pC, bn=bn: e.copy(out=Csb[:, 0:bn], in_=pC[:, 0:bn]), reads=[pC], writes=[Csb])
                    if not smp:
                        P.op("pool", lambda e, j=j: e.tensor_copy(out=prod[:, 0:2], in_=prodH[:, j, :]), reads=[prodH], writes=[prod])
                        P.op("dve", lambda e, pH=pH, bn=bn: e.tensor_tensor(out=prod[:, 2:2 + bn], in0=Csb[:, 0:bn], in1=pH[:, 0:bn], op=ALU.mult),
                             reads=[Csb, pH], writes=[prod])
                        if halo:
                            P.op("dve", lambda e: e.tensor_scalar(out=prod[:, 2:130], in0=prod[:, 2:130], scalar1=small[:, S_PM:S_PM + 1],
                                                                  scalar2=None, op0=ALU.mult), reads=[prod, small], writes=[prod])
                        conv3(ctmp[:, 0:bn], lambda o, bn=bn: prod[:, o:o + bn], wcv, j, bn, ctmp[:, 0:bn])
                        P.op("dve", lambda e, pB=pB, j=j, bn=bn: e.tensor_tensor(out=yT[:, j, 0:bn], in0=pB[:, 0:bn], in1=ctmp[:, 0:bn], op=ALU.mult),
                             reads=[pB, ctmp], writes=[yT])
                        P.op("pool", lambda e, j=j, bn=bn: e.tensor_copy(out=prodH[:, j, :], in_=prod[:, bn:bn + 2]), reads=[prod], writes=[prodH])
                    else:
                        P.op("pool", lambda e, j=j: e.tensor_copy(out=psx[:, :, 0:2], in_=sconvT[:, j, :, :]), reads=[sconvT], writes=[psx])
                        P.op("dve", lambda e, pH=pH: e.tensor_tensor(out=psx[:, :, 2:6], in0=Csb[:, 0:TS].rearrange("p (b t) -> p b t", t=4),
                                                                     in1=pH[:, 0:TS].rearrange("p (b t) -> p b t", t=4), op=ALU.mult),
                             reads=[Csb, pH], writes=[psx])
                        cv = ctmp[:, 0:TS].rearrange("p (b t) -> p b t", t=4)
                        conv3(cv, lambda o: psx[:, :, o:o + 4], wcv, j, 4, cv)
                        P.op("dve", lambda e, pB=pB, j=j, cv=cv, b0=b0: e.tensor_tensor(out=yT[:, j, b0:b0 + TS].rearrange("p (b t) -> p b t", t=4),
                                                                                in0=pB[:, 0:TS].rearrange("p (b t) -> p b t", t=4), in1=cv, op=ALU.mult),
                             reads=[pB, ctmp], writes=[yT])
                        P.op("pool", lambda e, j=j: e.tensor_copy(out=sconv_o[:, j, :, :], in_=psx[:, :, 4:6]), reads=[psx], writes=[sconv_o])
            _ck(P, 'E_' + str(p))
            ln_prefetch(0)
            proj_resid(w_out_r, KC, yT, tiles, G[0], G[1])
            for (ti, rows, col0, smp) in tiles:
                ln_affine(ti, rows, 0)
            _ck(P, 'F1_' + str(p))
            for (ti, rows, col0, smp) in tiles:
                ln_transpose(ti, rows, col0, 2, 3, smp)
            _ck(P, 'F2_' + str(p))
            P.handoff(mixer_bufs, ffn_bufs)
            for jj in range(JF // 2):
                wb = get_unit([(0, w_up_r[:, :, jj * 256:(jj + 1) * 256]), (256, w_up_r[:, :, DFF + jj * 256:DFF + (jj + 1) * 256])])
                for cj in range(2):
                    j = jj * 2 + cj
                    for (b0, bn, smp) in blocks:
                        pa = nxt("a", PA)
                        mm(pa, pa[:, 0:bn], [(wb[:, k, cj * 128:(cj + 1) * 128], uT[:, k, b0:b0 + bn]) for k in range(KC)], [uT, wb])
                        pg_ = nxt("a", PA)
                        mm(pg_, pg_[:, 0:bn], [(wb[:, k, 256 + cj * 128:256 + (cj + 1) * 128], uT[:, k, b0:b0 + bn]) for k in range(KC)], [uT, wb])
                        et = nxt("e", etmp)
                        if not smp:
                            P.op("pool", lambda e, j=j: e.tensor_copy(out=abuf[:, 0:2], in_=aH[:, j, :]), reads=[aH], writes=[abuf])
                            P.op("act", lambda e, pa=pa, bn=bn: e.copy(out=abuf[:, 2:2 + bn], in_=pa[:, 0:bn]), reads=[pa], writes=[abuf])
                            if halo:
                                P.op("dve", lambda e: e.tensor_scalar(out=abuf[:, 2:130], in0=abuf[:, 2:130], scalar1=small[:, S_PM:S_PM + 1],
                                                                      scalar2=None, op0=ALU.mult), reads=[abuf, small], writes=[abuf])
                            conv3(et[:, 0:bn], lambda o, bn=bn: abuf[:, o:o + bn], wfc, j, bn, et[:, 0:bn])
                            P.op("pool", lambda e, j=j, bn=bn: e.tensor_copy(out=aH[:, j, :], in_=abuf[:, bn:bn + 2]), reads=[abuf], writes=[aH])
                            P.op("act", lambda e, et=et, bn=bn: e.activation(out=et[:, 0:bn], in_=et[:, 0:bn], func=AF.Silu), reads=[et], writes=[et])
                            P.op("dve", lambda e, et=et, pg_=pg_, j=j, bn=bn: e.tensor_tensor(out=zT[:, j, 0:bn], in0=pg_[:, 0:bn], in1=et[:, 0:bn], op=ALU.mult),
                                 reads=[pg_, et], writes=[zT])
                        else:
                            P.op("pool", lambda e, j=j: e.tensor_copy(out=asx[:, :, 0:2], in_=sfcT[:, j, :, :]), reads=[sfcT], writes=[asx])
                            P.op("act", lambda e, pa=pa: e.copy(out=asx[:, :, 2:6], in_=pa[:, 0:TS].rearrange("p (b t) -> p b t", t=4)),
                                 reads=[pa], writes=[asx])
                            ev = et[:, 0:TS].rearrange("p (b t) -> p b t", t=4)
                            conv3(ev, lambda o: asx[:, :, o:o + 4], wfc, j, 4, ev)
                            P.op("pool", lambda e, j=j: e.tensor_copy(out=sfc_o[:, j, :, :], in_=asx[:, :, 4:6]), reads=[asx], writes=[sfc_o])
                            P.op("act", lambda e, et=et: e.activation(out=et[:, 0:TS], in_=et[:, 0:TS], func=AF.Silu), reads=[et], writes=[et])
                            P.op("dve", lambda e, et=et, pg_=pg_, j=j, b0=b0: e.tensor_tensor(out=zT[:, j, b0:b0 + TS], in0=pg_[:, 0:TS], in1=et[:, 0:TS], op=ALU.mult),
                                 reads=[pg_, et], writes=[zT])
            _ck(P, 'F3_' + str(p))
            ln_prefetch(1)
            proj_resid(w_down_r, JF, zT, tiles, G[2], G[3])
            for (ti, rows, col0, smp) in tiles:
                ln_affine(ti, rows, 1)
                if smp:
                    P.dma("sp", y_s, R[0:rows, ti, :], reads=[Rt[ti]], is_out=True)
                else:
                    row0 = tok0 + ti * 128 - 1024
                    if row0 >= 0:
                        P.dma("sp", y_p[row0:row0 + 128, :], R[0:rows, ti, :], reads=[Rt[ti]], is_out=True)
        _ck(P, 'final')
        P.dma("sp", o_pC, Cst[:, :, :, 0:256], reads=[Cst], is_out=True)
        P.op("act", lambda e: e.copy(out=sc[:, 0:8].rearrange("p (h c) -> p h c", c=2), in_=Cst[:, :, :, 256]), reads=[Cst], writes=[sc])
        P.op("dve", lambda e: e.tensor_tensor(out=sc[:, 8:12], in0=FM[:, 0:4], in1=FM[:, 4:8], op=ALU.add), reads=[FM], writes=[sc])
        P.dma("sp", o_pn, sc[:, 0:8], reads=[sc], is_out=True)
        P.dma("sp", o_pm, sc[:, 8:12], reads=[sc], is_out=True)
        P.dma("sp", o_pconv, prodH[:, :, :].rearrange("p a b -> p (a b)"), reads=[prodH], is_out=True)
        P.dma("sp", o_pfc, aH[:, :, :].rearrange("p a b -> p (a b)"), reads=[aH], is_out=True)
        P.dma("sp", o_sconv, sconv_o[:, :, :, :].rearrange("p a b c -> p (a b c)"), reads=[sconv_o], is_out=True)
        P.dma("sp", o_sfc, sfc_o[:, :, :, :].rearrange("p a b c -> p (a b c)"), reads=[sfc_o], is_out=True)
        P.emit()
        print("ops:", {e: len(v) for e, v in P.ops.items()})
    return nc


_NC = None


def kernel(x_prompt, x_sample, c_prompt, c_sample, state_conv, state_mlstm_C, state_mlstm_n, state_mlstm_m,
           state_ffn_conv, w_ada, b_ada, w_in, b_gate, w_conv, w_mh_norm, w_out, ln1_g, ln1_b, w_up, w_ffn_conv,
           w_down, ln2_g, ln2_b):
    global _NC
    f = lambda a: np.ascontiguousarray(np.asarray(a, dtype=np.float32))
    x_prompt, x_sample, c_prompt, c_sample = f(x_prompt), f(x_sample), f(c_prompt), f(c_sample)
    if _NC is None:
        _NC = build()
    nc = _NC
    con = make_consts()
    shared = {
        "con": con,
        "w_ada": f(w_ada[0]), "badaT": f(np.asarray(b_ada[0]).reshape(96, 128).T),
        "badarep": f(np.tile(np.concatenate([np.asarray(b_ada[0])[4096:6144], np.asarray(b_ada[0])[10240:12288]])[None, :], (128, 1))),
        "w_in": f(w_in[0]), "bgate": f(np.tile(np.asarray(b_gate[0])[None, :], (128, 1))),
        "wconvT": f(np.asarray(w_conv[0]).reshape(3, 8, 128).transpose(2, 1, 0).reshape(128, 24)),
        "wmhT": f(np.asarray(w_mh_norm[0]).reshape(8, 128).T),
        "w_out": f(w_out[0]), "w_up": f(w_up[0]),
        "wfcT": f(np.asarray(w_ffn_conv[0]).reshape(3, JF, 128).transpose(2, 1, 0).reshape(128, JF * 3)),
        "w_down": f(w_down[0]),
        "lnrep": f(np.stack([np.tile(np.asarray(v[0])[None, :], (128, 1)) for v in (ln1_g, ln1_b, ln2_g, ln2_b)])),
    }
    in_maps = []
    for c in range(8):
        bp = c // 2
        hh = c % 2
        sq = slice(16 * c, 16 * c + 16)
        if hh == 1:
            xpc = x_prompt[bp]
        else:
            xpc = np.concatenate([x_prompt[bp, 0:896], x_prompt[bp, 0:128], x_prompt[bp, 0:1024]], axis=0)
        call = np.concatenate([c_prompt[bp:bp + 1], c_sample[sq], np.repeat(c_prompt[bp:bp + 1], 128, 0),
                               np.repeat(c_sample[sq], 4, 0)], axis=0)
        cT = call.T.reshape(KC, 128, 209).transpose(1, 0, 2)
        m0 = np.asarray(state_mlstm_m[0])[sq]
        d = dict(shared)
        d.update({
            "xp": f(xpc), "pm": np.full((128, 1), float(hh), np.float32), "xs": f(x_sample[sq].reshape(TS, D)), "cT": f(cT),
            "m0tok": f(np.repeat(m0, 4, 0)), "m0rep": f(np.tile(m0.T.reshape(1, 64), (128, 1))),
            "C0": f(np.asarray(state_mlstm_C[0])[sq]),
            "n0T": f(np.asarray(state_mlstm_n[0])[sq].reshape(16, 4, 2, 128).transpose(3, 0, 1, 2).reshape(128, 128)),
            "sconvT": f(np.asarray(state_conv[0])[sq].reshape(16, 2, 8, 128).transpose(3, 2, 0, 1).reshape(128, 256)),
            "sfcT": f(np.asarray(state_ffn_conv[0])[sq].reshape(16, 2, JF, 128).transpose(3, 2, 0, 1).reshape(128, JF * 32)),
        })
        in_maps.append(d)
    res = run_bass_kernel_spmd(nc, in_maps, core_ids=list(range(8)))
    rs = res.results
    y_prompt = np.stack([np.concatenate([rs[2 * b]["y_p"], rs[2 * b + 1]["y_p"]], axis=0) for b in range(4)])
    y_sample = np.concatenate([rs[c]["y_s"].reshape(16, 4, D) for c in range(8)], 0)
    od = [1, 3, 5, 7]
    pconv = np.stack([rs[c]["o_pconv"].reshape(128, 8, 2).transpose(2, 1, 0).reshape(2, 1024) for c in od])[None]
    pC = np.stack([rs[c]["o_pC"].transpose(1, 2, 0, 3).reshape(4, 256, 256) for c in od])[None]
    pn = np.stack([rs[c]["o_pn"].reshape(128, 4, 2).transpose(1, 2, 0).reshape(4, 256) for c in od])[None]
    pm = np.stack([rs[c]["o_pm"][0] for c in od])[None]
    pfc = np.stack([rs[c]["o_pfc"].reshape(128, JF, 2).transpose(2, 1, 0).reshape(2, DFF) for c in od])[None]
    sconv = np.concatenate([rs[c]["o_sconv"].reshape(128, 8, 16, 2).transpose(2, 3, 1, 0).reshape(16, 2, 1024) for c in range(8)], 0)[None]
    sC = np.concatenate([rs[c]["o_sC"] for c in range(8)], 0)[None]
    sn = np.concatenate([rs[c]["o_sn"].reshape(128, 16, 4, 2).transpose(1, 2, 3, 0).reshape(16, 4, 256) for c in range(8)], 0)[None]
    sm = np.concatenate([rs[c]["o_sm"].reshape(16, 4, 4)[:, 3, :] for c in range(8)], 0)[None]
    sfc = np.concatenate([rs[c]["o_sfc"].reshape(128, JF, 16, 2).transpose(2, 3, 1, 0).reshape(16, 2, DFF) for c in range(8)], 0)[None]
    outs = (y_prompt, y_sample, pconv, pC, pn, pm, pfc, sconv, sC, sn, sm, sfc)
    return tuple(np.ascontiguousarray(o, dtype=np.float32) for o in outs)
```
